# Optimizing a Trainium2 kernel written in Bass

```python
import jax, jax.numpy as jnp
from jax import lax
import numpy as np

D_MODEL = 1024
BATCH = 8
SEQ = 4096
DEPTH = 2

D_MIX = D_MODEL
D_CONF = D_MIX // 2
D_SC = D_MIX - D_CONF
N_GROUPS_CONF = 8
N_GROUPS_SC = 8
CONF_KERNEL = 31
SC_KERNEL = 3
FFN_KERNEL = 3
D_FF = 2816
D_IN = 2 * D_CONF + 3 * D_SC
EPS = 1e-6

kernel_name = "hybrid_conformer_shortconv_convffn"


def rmsnorm(x, g):
    xf = x.astype(jnp.float32)
    y = xf * lax.rsqrt(jnp.mean(xf * xf, axis=-1, keepdims=True) + EPS)
    return (y * g.astype(jnp.float32)).astype(x.dtype)


def layernorm(x, g, b):
    xf = x.astype(jnp.float32)
    mu = jnp.mean(xf, axis=-1, keepdims=True)
    xc = xf - mu
    var = jnp.mean(xc * xc, axis=-1, keepdims=True)
    y = xc * lax.rsqrt(var + EPS)
    return (y * g.astype(jnp.float32) + b.astype(jnp.float32)).astype(x.dtype)


def causal_dwconv(x, w):
    k, c = w.shape
    return lax.conv_general_dilated(
        x, w[:, None, :].astype(x.dtype),
        window_strides=(1,), padding=[(k - 1, 0)],
        dimension_numbers=("NWC", "WIO", "NWC"),
        feature_group_count=c)


def setup_inputs(seed: int = 0) -> dict:
    key = jax.random.key(seed)
    ks = jax.random.split(key, 16)
    f32 = jnp.float32
    x = jax.random.normal(ks[0], (BATCH, SEQ, D_MODEL), f32)
    mix_norm_g = 1.0 + 0.02 * jax.random.normal(ks[1], (DEPTH, D_MODEL), f32)
    w_in = jax.random.normal(ks[2], (DEPTH, D_MODEL, D_IN), f32) * D_MODEL ** -0.5
    b_in = 0.02 * jax.random.normal(ks[3], (DEPTH, D_IN), f32)
    conv_a_w = jax.random.normal(ks[4], (DEPTH, CONF_KERNEL, D_CONF), f32) * CONF_KERNEL ** -0.5
    conv_a_b = 0.02 * jax.random.normal(ks[5], (DEPTH, D_CONF), f32)
    ln_a_g = 1.0 + 0.02 * jax.random.normal(ks[6], (DEPTH, D_CONF), f32)
    ln_a_b = 0.02 * jax.random.normal(ks[7], (DEPTH, D_CONF), f32)
    conv_b_w = jax.random.normal(ks[8], (DEPTH, SC_KERNEL, D_SC), f32) * SC_KERNEL ** -0.5
    w_out = jax.random.normal(ks[9], (DEPTH, D_MIX, D_MODEL), f32) * D_MIX ** -0.5
    ffn_norm_g = 1.0 + 0.02 * jax.random.normal(ks[10], (DEPTH, D_MODEL), f32)
    w_up = jax.random.normal(ks[11], (DEPTH, D_MODEL, 2 * D_FF), f32) * D_MODEL ** -0.5
    conv_f_w = jax.random.normal(ks[12], (DEPTH, FFN_KERNEL, 2 * D_FF), f32) * FFN_KERNEL ** -0.5
    w_down = jax.random.normal(ks[13], (DEPTH, D_FF, D_MODEL), f32) * D_FF ** -0.5
    final_norm_g = 1.0 + 0.02 * jax.random.normal(ks[14], (D_MODEL,), f32)
    return {"x": x, "mix_norm_g": mix_norm_g, "w_in": w_in, "b_in": b_in,
            "conv_a_w": conv_a_w, "conv_a_b": conv_a_b, "ln_a_g": ln_a_g, "ln_a_b": ln_a_b,
            "conv_b_w": conv_b_w, "w_out": w_out, "ffn_norm_g": ffn_norm_g, "w_up": w_up,
            "conv_f_w": conv_f_w, "w_down": w_down, "final_norm_g": final_norm_g}


def token_mixer(h, w_in, b_in, conv_a_w, conv_a_b, ln_a_g, ln_a_b, conv_b_w, w_out):
    u = jnp.einsum("bsd,de->bse", h, w_in) + b_in.astype(h.dtype)
    a_val, a_gate, g_b, g_c, v_sc = jnp.split(
        u, [D_CONF, 2 * D_CONF, 2 * D_CONF + D_SC, 2 * D_CONF + 2 * D_SC], axis=-1)
    a = a_val * jax.nn.sigmoid(a_gate)
    a = causal_dwconv(a, conv_a_w) + conv_a_b.astype(a.dtype)
    a = jax.nn.silu(layernorm(a, ln_a_g, ln_a_b))
    s = g_b * causal_dwconv(g_c * v_sc, conv_b_w)
    y = jnp.concatenate([a, s], axis=-1)
    return jnp.einsum("bse,ed->bsd", y, w_out)


def conv_ffn(h, w_up, conv_f_w, w_down):
    u = jnp.einsum("bsd,df->bsf", h, w_up)
    u = causal_dwconv(u, conv_f_w)
    gate, val = jnp.split(u, 2, axis=-1)
    return jnp.einsum("bsf,fd->bsd", jax.nn.silu(gate) * val, w_down)


def reference(x, mix_norm_g, w_in, b_in, conv_a_w, conv_a_b, ln_a_g, ln_a_b,
              conv_b_w, w_out, ffn_norm_g, w_up, conv_f_w, w_down, final_norm_g):
    for l in range(DEPTH):
        h = rmsnorm(x, mix_norm_g[l])
        x = x + token_mixer(h, w_in[l], b_in[l], conv_a_w[l], conv_a_b[l],
                            ln_a_g[l], ln_a_b[l], conv_b_w[l], w_out[l])
        h = rmsnorm(x, ffn_norm_g[l])
        x = x + conv_ffn(h, w_up[l], conv_f_w[l], w_down[l])
    return rmsnorm(x, final_norm_g)
```

```python
import numpy as np
import concourse.bass as bass
import concourse.mybir as mybir
from concourse.bass_utils import run_bass_kernel_spmd

F32 = mybir.dt.float32
BF16 = mybir.dt.bfloat16
AF = mybir.ActivationFunctionType
ALU = mybir.AluOpType

D = 1024
S = 4096
L = 2
DFF = 2816
T = 512
NT = S // T
NCH = 8
NF = 22
EPS = 1e-6
NSLOT = 5
SLOT = 2816

IN_ORDER = []
for _j in range(4):
    IN_ORDER += [_j, 4 + _j]
for _j in range(4):
    IN_ORDER += [12 + _j, 16 + _j]
IN_ORDER += [8, 9, 10, 11]

P_G1 = 0
P_BA = 8
P_CAW = 28
P_CAB = 152
P_LNG = 156
P_LNB = 160
P_CBW = 164
P_G2 = 176
P_CFW = 184
PL = 316
P_GF = 2 * PL
NPRM = 640


def _cols(v):
    v = np.asarray(v, dtype=np.float32).reshape(-1, 128)
    return v.T


def _prep_params(inp):
    cols = []
    for l in range(L):
        cols.append(_cols(inp["mix_norm_g"][l]))
        cols.append(_cols(inp["b_in"][l].reshape(20, 128)[IN_ORDER]))
        caw = inp["conv_a_w"][l].reshape(31, 4, 128).transpose(1, 0, 2)
        cols.append(_cols(caw))
        cols.append(_cols(inp["conv_a_b"][l]))
        cols.append(_cols(inp["ln_a_g"][l]))
        cols.append(_cols(inp["ln_a_b"][l]))
        cbw = inp["conv_b_w"][l].reshape(3, 4, 128).transpose(1, 0, 2)
        cols.append(_cols(cbw))
        cols.append(_cols(inp["ffn_norm_g"][l]))
        cfw = inp["conv_f_w"][l].reshape(3, 2, NF, 128).transpose(2, 1, 0, 3)
        cols.append(_cols(cfw))
    cols.append(_cols(inp["final_norm_g"]))
    prm = np.ascontiguousarray(np.concatenate(cols, axis=1), dtype=np.float32)
    assert prm.shape == (128, NPRM), prm.shape
    return prm


def _prep_weights(inp):
    w_in = np.asarray(inp["w_in"], np.float32)
    w = w_in.reshape(L, 8, 128, 20, 128)[:, :, :, IN_ORDER, :]
    w = w.reshape(L, 8, 128, 10, 2, 128).transpose(0, 3, 2, 1, 4, 5)
    wA = np.ascontiguousarray(w).reshape(L * 1280, 2048)
    w = np.asarray(inp["w_out"], np.float32).reshape(L, 8, 128, 4, 2, 128).transpose(0, 3, 2, 1, 4, 5)
    wO = np.ascontiguousarray(w).reshape(L * 512, 2048)
    w = np.asarray(inp["w_up"], np.float32).reshape(L, 8, 128, 2, NF, 128).transpose(0, 4, 2, 1, 3, 5)
    wU = np.ascontiguousarray(w).reshape(L * NF * 128, 2048)
    w = np.asarray(inp["w_down"], np.float32).reshape(L, NF, 128, 8, 128).transpose(0, 3, 2, 1, 4)
    wD = np.ascontiguousarray(w).reshape(L * 1024, 2816)
    return wA, wO, wU, wD


class Sem:
    def __init__(self, h, name):
        self.h = h
        self.name = name
        self.total = 0


class Eng:
    def __init__(self, h, sem, name):
        self.h = h
        self.sem = sem
        self.name = name
        self.n = 0
        self.seen = {}

    def wait(self, tok):
        if tok is None:
            return
        sem, val = tok
        if self.seen.get(sem.name, 0) >= val:
            return
        self.h.wait_ge(sem.h, val)
        self.seen[sem.name] = val

    def sig(self, ins):
        self.n += 1
        ins.then_inc(self.sem.h, 1)
        return (self.sem, self.n)

    def last(self):
        return (self.sem, self.n) if self.n else None


class Buf:
    def __init__(self):
        self.w = None
        self.r = {}


def acquire(eng, reads=(), writes=()):
    for b in reads:
        eng.wait(b.w)
    for b in writes:
        eng.wait(b.w)
        for t in b.r.values():
            eng.wait(t)


def release(tok, who, reads=(), writes=()):
    for b in reads:
        b.r[who] = tok
    for b in writes:
        b.w = tok
        b.r = {}


def build_program():
    nc = bass.Bass("TRN2", target_bir_lowering=False)
    dt = nc.dram_tensor
    xT = dt("xT", [D, S], F32, kind="ExternalInput").ap()
    wA = dt("wA", [L * 1280, 2048], F32, kind="ExternalInput").ap()
    wO = dt("wO", [L * 512, 2048], F32, kind="ExternalInput").ap()
    wU = dt("wU", [L * NF * 128, 2048], F32, kind="ExternalInput").ap()
    wD = dt("wD", [L * 1024, 2816], F32, kind="ExternalInput").ap()
    prm_d = dt("prm", [128, NPRM], F32, kind="ExternalInput").ap()
    ident_d = dt("ident", [128, 128], F32, kind="ExternalInput").ap()
    outT = dt("outT", [D, S], F32, kind="ExternalOutput").ap()
    sA = dt("sA", [L * 1280, 2048], BF16, kind="Internal").ap()
    sO = dt("sO", [L * 512, 2048], BF16, kind="Internal").ap()
    sU = dt("sU", [L * NF * 128, 2816], BF16, kind="Internal").ap()
    sD = dt("sD", [L * 1024, 2816], BF16, kind="Internal").ap()
    sCA = dt("sCA", [L * 1024, 2048], BF16, kind="Internal").ap()
    sCB = dt("sCB", [L * 128, 1536], BF16, kind="Internal").ap()

    xT_v = xT.rearrange("(c p) t -> p c t", p=128)
    outT_v = outT.rearrange("(c p) t -> p c t", p=128)

    import contextlib
    es = contextlib.ExitStack()
    with es:
        def sb(name, shape, dtype):
            return es.enter_context(nc.sbuf_tensor(name, shape, dtype))

        def mksem(name):
            return Sem(es.enter_context(nc.semaphore(name)), name)

        xs = sb("xs", [128, NT * NCH * T], F32)
        ring = sb("ring", [128, NSLOT * SLOT], BF16)
        hbuf = sb("hbuf", [128, NCH, T], BF16)
        sqb_t = sb("sqb", [128, 2, T], BF16)
        prm = sb("prm_sb", [128, NPRM], F32)
        ident = sb("ident_sb", [128, 128], F32)
        ones1024 = sb("ones1024", [128, 128], BF16)
        ones512 = sb("ones512", [128, 128], BF16)
        epsc = sb("epsc", [128, 1], F32)
        sd_t = sb("sd", [128, T], F32)
        var_t = sb("var", [128, T], F32)
        stg = sb("stg", [128, 17408], BF16)
        ps = es.enter_context(nc.psum_tensor("ps", [128, 8, T], F32))

        def xt(i):
            return xs[:, i * NCH * T:(i + 1) * NCH * T].rearrange("p (c t) -> p c t", c=NCH)

        o = 0
        a_in = stg[:, o:o + 4 * 542].rearrange("p (j t) -> p j t", j=4); o += 4 * 542
        cx = stg[:, o:o + 4 * 514].rearrange("p (j t) -> p j t", j=4); o += 4 * 514
        gb = stg[:, o:o + 4 * T].rearrange("p (j t) -> p j t", j=4); o += 4 * T
        ac_bf = stg[:, o:o + 4 * T].rearrange("p (j t) -> p j t", j=4); o += 4 * T
        asq = stg[:, o:o + 4 * T].rearrange("p (j t) -> p j t", j=4); o += 4 * T
        ybuf = stg[:, o:o + 8 * T].rearrange("p (j t) -> p j t", j=8); o += 8 * T
        tmpM = [stg[:, o + k * 2 * T:o + (k + 1) * 2 * T].bitcast(F32) for k in range(2)]; o += 4 * T
        assert o <= 17408, o
        o = 0
        gbuf = stg[:, o:o + NF * T].rearrange("p (j t) -> p j t", j=NF); o += NF * T
        halu = stg[:, o:o + NF * 8].bitcast(F32).rearrange("p (q t) -> p q t", t=2); o += NF * 8
        contrib = stg[:, o:o + NF * 8].bitcast(F32).rearrange("p (q t) -> p q t", t=2); o += NF * 8
        tmp44 = stg[:, o:o + NF * 4].bitcast(F32); o += NF * 4
        accg = [stg[:, o + k * 2 * T:o + (k + 1) * 2 * T].bitcast(F32) for k in range(3)]; o += 6 * T
        accv = [stg[:, o + k * 2 * T:o + (k + 1) * 2 * T].bitcast(F32) for k in range(2)]; o += 4 * T
        assert o <= 17408, o

        PE = Eng(nc.tensor, mksem("s_pe"), "PE")
        ACT = Eng(nc.scalar, mksem("s_act"), "ACT")
        DVE = Eng(nc.vector, mksem("s_dve"), "DVE")
        POOL = Eng(nc.gpsimd, mksem("s_pool"), "POOL")
        SP = Eng(nc.sync, mksem("s_sp"), "SP")
        engines = [PE, ACT, DVE, POOL, SP]

        def op(eng, fn, reads=(), writes=()):
            acquire(eng, reads, writes)
            tok = eng.sig(fn())
            release(tok, eng.name, reads, writes)
            return tok

        def dma(q, sem, out, in_, reads=(), writes=(), extra=()):
            acquire(q, reads, writes)
            for t in extra:
                q.wait(t)
            q.h.dma_start(out=out, in_=in_).then_inc(sem.h, 16)
            sem.total += 16
            tok = (sem, sem.total)
            release(tok, "dma:" + sem.name, reads, writes)
            return tok

        xb = [Buf() for _ in range(NT)]
        slotb = [Buf() for _ in range(NSLOT)]
        hb = Buf()
        sqbb = [Buf(), Buf()]
        pb = [Buf() for _ in range(8)]
        sdb, rstdb, meanb, varb = Buf(), Buf(), Buf(), Buf()
        ainb = [Buf() for _ in range(4)]
        cxb = [Buf() for _ in range(4)]
        gbb = [Buf() for _ in range(4)]
        acb = [Buf() for _ in range(4)]
        asqb = [Buf() for _ in range(4)]
        yb = [Buf() for _ in range(8)]
        tmpb = [Buf(), Buf()]
        gfb = [Buf() for _ in range(NF)]
        ubm = [Buf() for _ in range(4)]
        ubh = [Buf() for _ in range(4)]
        halb = [Buf() for _ in range(NF)]
        halvb = [Buf() for _ in range(NF)]
        accgb = [Buf(), Buf(), Buf()]
        accvb = [Buf(), Buf()]
        halub = [Buf() for _ in range(2 * NF)]
        contribb = Buf()
        constb = Buf()
        prmb = Buf()
        hbc = [Buf() for _ in range(NCH)]
        pe_pend = []
        state = {"bank": 0, "unit": 0, "tmp": 0}

        def next_tmp():
            k = state["tmp"]
            state["tmp"] = (k + 1) % 2
            return k

        s_slot = [mksem(f"s_slot{k}") for k in range(NSLOT)]

        def next_unit(src, n, deps):
            k = state["unit"] % NSLOT
            state["unit"] += 1
            dst = ring[:, k * SLOT:k * SLOT + n]
            dma(SP, s_slot[k], dst, src, writes=[slotb[k]], extra=deps)
            return ring[:, k * SLOT:(k + 1) * SLOT], slotb[k]

        def mm(b, lhsT, rhs, start, stop, reads, signal):
            acquire(PE, reads, [pb[b]] if start else [])
            ins = nc.tensor.matmul(ps[:, b, :], lhsT=lhsT, rhs=rhs, start=start, stop=stop)
            pe_pend.extend(reads)
            if signal or stop:
                tok = PE.sig(ins)
                release(tok, "PE", list(pe_pend), [pb[b]] if stop else [])
                del pe_pend[:]

        def pcol(c):
            return prm[:, c:c + 1]

        reserved = set()

        def alloc_bank():
            while True:
                b = state["bank"]
                state["bank"] = (b + 1) % 8
                if b not in reserved:
                    return b

        s_c = mksem("s_const")
        s_x = [mksem(f"s_x{i}") for i in range(NT)]
        t_prm = dma(SP, s_c, prm[:], prm_d[:], writes=[prmb])
        t_prm = dma(SP, s_c, ident[:], ident_d[:], writes=[prmb])
        NX0 = 3
        dma(ACT, s_x[0], xt(0), xT_v[:, :, 0:T], writes=[xb[0]])
        op(DVE, lambda: nc.vector.memset(ones1024[:], 1.0 / 1024.0), writes=[constb])
        op(DVE, lambda: nc.vector.memset(ones512[:], 1.0 / 512.0), writes=[constb])
        op(DVE, lambda: nc.vector.memset(epsc[:], EPS), writes=[constb])

        cast_tokens = {}

        def cast_piece(name, l, u0, u1):
            sem = mksem(f"s_cast_{name}{l}_{u0}")
            if name == "A":
                r0, r1 = (l * 10 + u0) * 128, (l * 10 + u1) * 128
                dst, src = sA[r0:r1, :], wA[r0:r1, :]
            elif name == "O":
                r0, r1 = (l * 4 + u0) * 128, (l * 4 + u1) * 128
                dst, src = sO[r0:r1, :], wO[r0:r1, :]
            elif name == "U":
                r0, r1 = (l * NF + u0) * 128, (l * NF + u1) * 128
                dst, src = sU[r0:r1, 0:2048], wU[r0:r1, :]
            else:
                r0, r1 = (l * 8 + u0) * 128, (l * 8 + u1) * 128
                dst = sD[r0:r1, :].rearrange("r (two h) -> (r two) h", two=2)
                src = wD[r0:r1, :].rearrange("r (two h) -> (r two) h", two=2)
            POOL.h.dma_start(out=dst, in_=src).then_inc(sem.h, 16)
            for u in range(u0, u1):
                cast_tokens[(name, l, u)] = (sem, 16)

        def pieces_of(name, l, n):
            return [(name, l, a, min(a + 2, n)) for a in range(0, n, 2)]

        cast_sched = {}
        pieces0 = pieces_of("U", 0, NF) + pieces_of("D", 0, 8) + pieces_of("A", 1, 10) + pieces_of("O", 1, 4)
        pieces1 = pieces_of("U", 1, NF) + pieces_of("D", 1, 8)
        for si_, pcs in ((0, pieces0), (2, pieces1)):
            per = -(-len(pcs) // (NT * 3))
            k = 0
            for tl in range(NT):
                for pt in range(3):
                    cast_sched[(si_, tl, pt)] = pcs[k:k + per]
                    k += per
            assert k >= len(pcs)

        POOL.wait((s_x[0], 16))
        for a in range(0, 10, 2):
            cast_piece("A", 0, a, a + 2)
        dma(POOL, s_x[1], xt(1), xT_v[:, :, T:2 * T], writes=[xb[1]])
        cast_piece("O", 0, 0, 2)
        cast_piece("O", 0, 2, 4)
        dma(POOL, s_x[2], xt(2), xT_v[:, :, 2 * T:3 * T], writes=[xb[2]])

        XO = NX0 * NCH * T
        R1 = xs[:, XO:XO + 8192].bitcast(BF16)
        RBr = xs[:, XO + 8192:XO + 8192 + 768].bitcast(BF16)
        RFr = xs[:, XO + 8192 + 768:XO + 8192 + 768 + 8448].bitcast(BF16)
        r1b, rbb, rfb = Buf(), Buf(), Buf()
        r1a = Buf()
        RA = R1.rearrange("p (j k n) -> p j k n", j=4, k=32)
        RB = RBr.rearrange("p (k n) -> p k n", k=12)
        RF = RFr.rearrange("p (k n) -> p k n", k=132)
        s_dg = {}

        def dg_tok(name, l):
            sem = s_dg[(name, l)]
            return (sem, sem.total)

        def bcast_build(eng, out, c0, n):
            return eng.h.tensor_tensor(out=out, in0=ident[:].unsqueeze(1).to_broadcast([128, n, 128]),
                                       in1=prm[:, c0:c0 + n].unsqueeze(2).to_broadcast([128, n, 128]), op=ALU.mult)

        def build_A(eng, l, reg, regb, js=(0, 1, 2, 3), store=True, regb2=None):
            ra = reg.rearrange("p (j k n) -> p j k n", j=4, k=32)
            acquire(eng, [constb, prmb], [regb])
            ins = None
            for j in js:
                bcast_build(eng, ra[:, j, 0:31, :], l * PL + P_CAW + j * 31, 31)
                ins = eng.h.memset(ra[:, j, 31, :], 0.0)
            release(eng.sig(ins), eng.name, [constb, prmb], [regb])
            if not store:
                return
            if regb2 is not None:
                POOL.wait(regb2.w)
            sem = mksem(f"s_dgA{l}")
            s_dg[("A", l)] = sem
            dma(POOL, sem, sCA[l * 1024:(l + 1) * 1024, :].rearrange("(u p) n -> p u n", p=128),
                reg.rearrange("p (u n) -> p u n", u=8), reads=[regb])

        def build_B(eng, l):
            acquire(eng, [constb, prmb], [rbb])
            ins = bcast_build(eng, RB, l * PL + P_CBW, 12)
            release(eng.sig(ins), eng.name, [constb, prmb], [rbb])
            sem = mksem(f"s_dgB{l}")
            s_dg[("B", l)] = sem
            dma(POOL, sem, sCB[l * 128:(l + 1) * 128, :], RBr, reads=[rbb])

        def build_F(eng, l):
            acquire(eng, [constb, prmb], [rfb])
            ins = bcast_build(eng, RF, l * PL + P_CFW, 132)
            release(eng.sig(ins), eng.name, [constb, prmb], [rfb])
            sem = mksem(f"s_dgF{l}")
            s_dg[("F", l)] = sem
            r0, r1 = l * NF * 128, (l + 1) * NF * 128
            dma(POOL, sem, sU[r0:r1, 2048:2816].rearrange("(j p) n -> p j n", p=128),
                RFr.rearrange("p (j n) -> p j n", j=NF), reads=[rfb])

        def prologue_builds():
            build_A(DVE, 0, R1, r1a, js=(0, 1), store=False)
            build_B(DVE, 0)
            build_A(POOL, 0, R1, r1b, js=(2, 3), store=True, regb2=r1a)
            for i in (3, 4):
                dma(POOL, s_x[i], xt(i), xT_v[:, :, i * T:(i + 1) * T], writes=[xb[i], r1b, r1a])
            build_A(POOL, 1, RFr[:, 0:16384], rfb)
            build_B(POOL, 1)
            for i in (5, 6, 7):
                dma(POOL, s_x[i], xt(i), xT_v[:, :, i * T:(i + 1) * T], writes=[xb[i], rbb, rfb])

        pend = {}
        late_unreserve = []

        def flush_unreserve():
            while late_unreserve:
                reserved.discard(late_unreserve.pop())

        def norm_p1_steps(key, i):
            x_i = xt(i)

            def step(k):
                if k < NCH:
                    q = k % 2
                    op(ACT, lambda: nc.scalar.activation(out=sqb_t[:, q, :], in_=x_i[:, k, :], func=AF.Square),
                       reads=[xb[i]], writes=[sqbb[q]])
                if k >= 1:
                    c = k - 1
                    if c == 0:
                        b = alloc_bank()
                        reserved.add(b)
                        pend[key] = b
                    b = pend[key]
                    q = c % 2
                    mm(b, ones1024[:], sqb_t[:, q, :], c == 0, c == NCH - 1, [sqbb[q], constb, prmb], True)
            return [(lambda k=k: step(k)) for k in range(NCH + 1)]

        def norm_p2(key, i, gcol, out_fn, out_bufs_w):
            b = pend.pop(key)
            x_i = xt(i)
            op(ACT, lambda: nc.scalar.activation(out=sd_t[:], in_=ps[:, b, :], func=AF.Sqrt, bias=epsc[:, 0:1], scale=1.0),
               reads=[pb[b], constb, prmb], writes=[sdb])
            op(DVE, lambda: nc.vector.reciprocal(out=ps[:, b, :], in_=sd_t[:]), reads=[sdb], writes=[pb[b]])
            for c in range(NCH):
                op(DVE, lambda: nc.vector.scalar_tensor_tensor(out=out_fn(c), in0=x_i[:, c, :], scalar=pcol(gcol + c),
                                                               in1=ps[:, b, :], op0=ALU.mult, op1=ALU.mult),
                   reads=[xb[i], pb[b], constb, prmb], writes=out_bufs_w(c))
            late_unreserve.append(b)

        def h_out(c):
            return hbuf[:, c, :]

        def issue_casts(si, i, pt):
            pcs = cast_sched.get((si, i, pt), [])
            if pcs:
                POOL.wait(PE.last())
                for pc in pcs:
                    cast_piece(*pc)

        def mixer_tile(l, i, steps, p2s):
            base = l * PL
            steps = list(steps)

            def inproj_unit(u):
                slot, sbf = next_unit(sA[(l * 10 + u) * 128:(l * 10 + u + 1) * 128, :], 2048, [cast_tokens[("A", l, u)]])
                W = slot[:, 0:2048].rearrange("p (c n) -> p c n", c=NCH)
                banks = []
                for s in range(2):
                    b = alloc_bank()
                    for c in range(NCH):
                        mm(b, W[:, c, s * 128:(s + 1) * 128], hbuf[:, c, :], c == 0, c == NCH - 1, [sbf, hbc[c]], False)
                    banks.append(b)
                bA, bB = banks
                c0 = base + P_BA + 2 * u
                if u < 8:
                    j = u % 4
                    k = next_tmp()
                    fn = AF.Sigmoid if u < 4 else AF.Identity
                    op(ACT, lambda: nc.scalar.activation(out=tmpM[k][:], in_=ps[:, bB, :], func=fn, bias=pcol(c0 + 1), scale=1.0),
                       reads=[pb[bB], constb, prmb], writes=[tmpb[k]])
                    if u < 4:
                        dst, dbuf = a_in[:, j, 30:30 + T], ainb[j]
                    else:
                        dst, dbuf = cx[:, j, 2:2 + T], cxb[j]
                    op(DVE, lambda: nc.vector.scalar_tensor_tensor(out=dst, in0=ps[:, bA, :], scalar=pcol(c0), in1=tmpM[k][:],
                                                                   op0=ALU.add, op1=ALU.mult),
                       reads=[pb[bA], tmpb[k], constb, prmb], writes=[dbuf])
                else:
                    for s in range(2):
                        j = (u - 8) * 2 + s
                        bb = banks[s]
                        op(ACT, lambda: nc.scalar.activation(out=gb[:, j, :], in_=ps[:, bb, :], func=AF.Identity, bias=pcol(c0 + s), scale=1.0),
                           reads=[pb[bb], constb, prmb], writes=[gbb[j]])

            for u in range(4):
                inproj_unit(u)
            for j in range(4):
                b = alloc_bank()
                for half in range(2):
                    uu = l * 8 + j * 2 + half
                    slot, sbf = next_unit(sCA[uu * 128:(uu + 1) * 128, :], 2048, [dg_tok("A", l)])
                    ntap = 16 if half == 0 else 15
                    for kk in range(ntap):
                        kt = half * 16 + kk
                        mm(b, slot[:, kk * 128:(kk + 1) * 128], a_in[:, j, kt:kt + T], kt == 0, kt == 30,
                           [sbf, ainb[j]], kk == ntap - 1)
                        if steps and kk in (4, 9, 14):
                            steps.pop(0)()
                cb = base + P_CAB + j
                op(ACT, lambda: nc.scalar.activation(out=ac_bf[:, j, :], in_=ps[:, b, :], func=AF.Identity, bias=pcol(cb), scale=1.0),
                   reads=[pb[b], constb, prmb], writes=[acb[j]])
                op(ACT, lambda: nc.scalar.activation(out=asq[:, j, :], in_=ps[:, b, :], func=AF.Square, bias=pcol(cb), scale=1.0),
                   reads=[pb[b], constb, prmb], writes=[asqb[j]])
            while steps:
                steps.pop(0)()
            if i + 1 < NT:
                op(DVE, lambda: nc.vector.tensor_copy(out=a_in[:, :, 0:30], in_=a_in[:, :, T:T + 30]), reads=[], writes=ainb)
            issue_casts(cur[0][0], cur[0][1], 1)
            bm = alloc_bank()
            reserved.add(bm)
            bq = alloc_bank()
            reserved.add(bq)
            for j in range(4):
                mm(bm, ones512[:], ac_bf[:, j, :], j == 0, j == 3, [acb[j], constb, prmb], False)
                mm(bq, ones512[:], asq[:, j, :], j == 0, j == 3, [asqb[j], constb, prmb], False)

            def ln_head():
                op(ACT, lambda: nc.scalar.activation(out=var_t[:], in_=ps[:, bm, :], func=AF.Square), reads=[pb[bm]], writes=[varb])
                op(DVE, lambda: nc.vector.tensor_tensor(out=var_t[:], in0=ps[:, bq, :], in1=var_t[:], op=ALU.subtract),
                   reads=[pb[bq], varb], writes=[varb])
                op(ACT, lambda: nc.scalar.activation(out=sd_t[:], in_=var_t[:], func=AF.Sqrt, bias=epsc[:, 0:1], scale=1.0),
                   reads=[varb, constb, prmb], writes=[sdb])
                op(DVE, lambda: nc.vector.reciprocal(out=ps[:, bq, :], in_=sd_t[:]), reads=[sdb], writes=[pb[bq]])

            def ln_j(j):
                k = next_tmp()
                op(DVE, lambda: nc.vector.tensor_tensor(out=tmpM[k][:], in0=ac_bf[:, j, :], in1=ps[:, bm, :], op=ALU.subtract),
                   reads=[acb[j], pb[bm]], writes=[tmpb[k]])
                op(DVE, lambda: nc.vector.tensor_tensor(out=tmpM[k][:], in0=tmpM[k][:], in1=ps[:, bq, :], op=ALU.mult),
                   reads=[tmpb[k], pb[bq]], writes=[tmpb[k]])
                op(ACT, lambda: nc.scalar.activation(out=ybuf[:, j, :], in_=tmpM[k][:], func=AF.Silu,
                                                     bias=pcol(base + P_LNB + j), scale=pcol(base + P_LNG + j)),
                   reads=[tmpb[k], constb, prmb], writes=[yb[j]])

            for u in range(4, 10):
                inproj_unit(u)
                if u == 4:
                    ln_head()
                elif u <= 8:
                    ln_j(u - 5)
            reserved.discard(bm)
            reserved.discard(bq)
            issue_casts(cur[0][0], cur[0][1], 2)
            slot, sbf = next_unit(sCB[l * 128:(l + 1) * 128, :], 1536, [dg_tok("B", l)])
            for j in range(4):
                b = alloc_bank()
                for kt in range(3):
                    mm(b, slot[:, (j * 3 + kt) * 128:(j * 3 + kt + 1) * 128], cx[:, j, kt:kt + T], kt == 0, kt == 2,
                       [sbf, cxb[j]], False)
                op(DVE, lambda: nc.vector.tensor_tensor(out=ybuf[:, 4 + j, :], in0=ps[:, b, :], in1=gb[:, j, :], op=ALU.mult),
                   reads=[pb[b], gbb[j]], writes=[yb[4 + j]])
            if i + 1 < NT:
                op(ACT, lambda: nc.scalar.activation(out=cx[:, :, 0:2], in_=cx[:, :, T:T + 2], func=AF.Identity), reads=[], writes=cxb)
            for f in p2s:
                f()
            x_i = xt(i)
            for u in range(4):
                slot, sbf = next_unit(sO[(l * 4 + u) * 128:(l * 4 + u + 1) * 128, :], 2048, [cast_tokens[("O", l, u)]])
                W = slot[:, 0:2048].rearrange("p (c n) -> p c n", c=NCH)
                for s in range(2):
                    oc = 2 * u + s
                    b = alloc_bank()
                    for c in range(NCH):
                        mm(b, W[:, c, s * 128:(s + 1) * 128], ybuf[:, c, :], c == 0, c == NCH - 1, [sbf, yb[c]], False)
                    op(DVE, lambda: nc.vector.tensor_tensor(out=x_i[:, oc, :], in0=ps[:, b, :], in1=x_i[:, oc, :], op=ALU.add),
                       reads=[pb[b]], writes=[xb[i]])

        def ffn_tile(l, i, steps, p2s, tail):
            units = {}
            steps = list(steps)

            cf = prm[:, l * PL + P_CFW:l * PL + P_CFW + 6 * NF].rearrange("p (q k) -> p q k", k=3)
            op(DVE, lambda: nc.vector.tensor_tensor(out=contrib[:, :, 1], in0=halu[:, :, 1], in1=cf[:, :, 0], op=ALU.mult),
               reads=halub + [prmb], writes=[contribb])
            op(DVE, lambda: nc.vector.tensor_tensor(out=contrib[:, :, 0], in0=halu[:, :, 0], in1=cf[:, :, 0], op=ALU.mult),
               reads=halub + [prmb], writes=[contribb])
            op(DVE, lambda: nc.vector.tensor_tensor(out=tmp44, in0=halu[:, :, 1], in1=cf[:, :, 1], op=ALU.mult),
               reads=halub + [prmb], writes=[contribb])
            op(DVE, lambda: nc.vector.tensor_tensor(out=contrib[:, :, 0], in0=contrib[:, :, 0], in1=tmp44, op=ALU.add),
               reads=[contribb], writes=[contribb])

            def conv_taps(b, acc, accbuf, q, c0):
                op(ACT, lambda: nc.scalar.activation(out=acc[:, 0:T], in_=ps[:, b, :], func=AF.Identity, scale=pcol(c0 + 2)),
                   reads=[pb[b], prmb], writes=[accbuf])
                op(ACT, lambda: nc.scalar.activation(out=halu[:, q, :], in_=ps[:, b, T - 2:T], func=AF.Identity),
                   reads=[pb[b]], writes=[halub[q]])
                op(DVE, lambda: nc.vector.scalar_tensor_tensor(out=acc[:, 1:T], in0=ps[:, b, 0:T - 1], scalar=pcol(c0 + 1),
                                                               in1=acc[:, 1:T], op0=ALU.mult, op1=ALU.add),
                   reads=[pb[b], accbuf, prmb], writes=[accbuf])
                op(DVE, lambda: nc.vector.scalar_tensor_tensor(out=acc[:, 2:T], in0=ps[:, b, 0:T - 2], scalar=pcol(c0),
                                                               in1=acc[:, 2:T], op0=ALU.mult, op1=ALU.add),
                   reads=[pb[b], accbuf, prmb], writes=[accbuf])
                op(POOL, lambda: nc.gpsimd.tensor_tensor(out=acc[:, 0:2], in0=acc[:, 0:2], in1=contrib[:, q, :], op=ALU.add),
                   reads=[contribb, accbuf], writes=[accbuf])

            def up(j):
                slot, sbf = next_unit(sU[(l * NF + j) * 128:(l * NF + j + 1) * 128, 0:2048], 2048, [cast_tokens[("U", l, j)]])
                W = slot[:, 0:2048].rearrange("p (c n) -> p c n", c=NCH)
                bks = []
                for s_ in range(2):
                    b = alloc_bank()
                    for c in range(NCH):
                        mm(b, W[:, c, s_ * 128:(s_ + 1) * 128], hbuf[:, c, :], c == 0, c == NCH - 1, [sbf, hbc[c]], False)
                    bks.append(b)
                bG, bV = bks
                kg, kv = j % 3, j % 2
                c0 = l * PL + P_CFW + j * 6
                conv_taps(bG, accg[kg], accgb[kg], 2 * j, c0)
                conv_taps(bV, accv[kv], accvb[kv], 2 * j + 1, c0 + 3)
                units[j] = (kg, kv)

            def conv(j):
                kg, kv = units.pop(j)
                op(ACT, lambda: nc.scalar.activation(out=accg[kg][:], in_=accg[kg][:], func=AF.Silu),
                   reads=[accgb[kg]], writes=[accgb[kg]])
                op(POOL, lambda: nc.gpsimd.tensor_tensor(out=gbuf[:, j, :], in0=accv[kv][:], in1=accg[kg][:], op=ALU.mult),
                   reads=[accvb[kv], accgb[kg]], writes=[gfb[j]])

            for j in range(NF + 1):
                if j < NF:
                    up(j)
                if j >= 1:
                    conv(j - 1)
                if steps and j >= 2:
                    steps.pop(0)()
            while steps:
                steps.pop(0)()
            for f in p2s:
                f()
            x_i = xt(i)
            for oc in range(NCH):
                slot, sbf = next_unit(sD[(l * 8 + oc) * 128:(l * 8 + oc + 1) * 128, :], 2816, [cast_tokens[("D", l, oc)]])
                b = alloc_bank()
                for fc in range(NF):
                    mm(b, slot[:, fc * 128:(fc + 1) * 128], gbuf[:, fc, :], fc == 0, fc == NF - 1, [sbf, gfb[fc]], False)
                op(DVE, lambda: nc.vector.tensor_tensor(out=x_i[:, oc, :], in0=ps[:, b, :], in1=x_i[:, oc, :], op=ALU.add),
                   reads=[pb[b]], writes=[xb[i]])
            for f in tail:
                f()

        def barrier():
            toks = [e.last() for e in engines]
            for e in engines:
                for t in toks:
                    if t is not None and t[0] is not e.sem:
                        e.wait(t)

        stages = []
        for l in range(L):
            stages.append(("M", l, l * PL + P_G1))
            stages.append(("F", l, l * PL + P_G2))

        s_out = mksem("s_out")

        def final_p2(i):
            x_i = xt(i)
            norm_p2(("fin", i), i, P_GF, lambda c: x_i[:, c, :], lambda c: [xb[i]])

        def final_store(i):
            dma(SP, s_out, outT_v[:, :, i * T:(i + 1) * T], xt(i), reads=[xb[i]])

        prologue_builds()
        for f in norm_p1_steps((0, 0), 0):
            f()
        norm_p2((0, 0), 0, stages[0][2], h_out, lambda c: [hbc[c]])
        flush_unreserve()
        last_si = len(stages) - 1
        cur = [None]
        for si, (kind, l, gcol) in enumerate(stages):
            if kind == "M":
                op(DVE, lambda: nc.vector.memset(a_in[:, :, 0:30], 0.0), writes=ainb)
                op(DVE, lambda: nc.vector.memset(cx[:, :, 0:2], 0.0), writes=cxb)
            else:
                op(DVE, lambda: nc.vector.memset(halu, 0.0), writes=halub)
            for i in range(NT):
                cur[0] = (si, i)
                issue_casts(si, i, 0)
                steps, p2s, tail = [], [], []
                if i + 1 < NT:
                    nk, ni, ng = (si, i + 1), i + 1, gcol
                elif si + 1 < len(stages):
                    nk, ni, ng = (si + 1, 0), 0, stages[si + 1][2]
                else:
                    nk = None
                if nk is not None:
                    steps += norm_p1_steps(nk, ni)
                    p2s.append(lambda nk=nk, ni=ni, ng=ng: norm_p2(nk, ni, ng, h_out, lambda c: [hbc[c]]))
                if si == last_si and i >= 1:
                    steps += norm_p1_steps(("fin", i - 1), i - 1)
                    p2s.append(lambda i=i: final_p2(i - 1))
                    tail.append(lambda i=i: final_store(i - 1))
                if kind == "M":
                    mixer_tile(l, i, steps, p2s)
                else:
                    ffn_tile(l, i, steps, p2s, tail)
                flush_unreserve()
            barrier()
        for f in norm_p1_steps(("fin", NT - 1), NT - 1):
            f()
        final_p2(NT - 1)
        final_store(NT - 1)
        SP.wait((s_out, s_out.total))
    return nc


_CACHE = {}


def kernel(**inputs):
    x = np.asarray(inputs["x"], np.float32)
    B = x.shape[0]
    assert x.shape == (8, S, D)
    prm = _prep_params({k: np.asarray(v, np.float32) for k, v in inputs.items() if k not in ("x",)})
    wA, wO, wU, wD = _prep_weights(inputs)
    ident = np.eye(128, dtype=np.float32)
    if "nc" not in _CACHE:
        _CACHE["nc"] = build_program()
    nc = _CACHE["nc"]
    in_maps = []
    for b in range(B):
        in_maps.append({"xT": np.ascontiguousarray(x[b].T), "wA": wA, "wO": wO, "wU": wU, "wD": wD,
                        "prm": prm, "ident": ident})
    res = run_bass_kernel_spmd(nc, in_maps, core_ids=list(range(B)))
    out = np.stack([np.ascontiguousarray(res.results[b]["outT"].T) for b in range(B)], axis=0)
    return out.astype(np.float32)
```

```python
import numpy as np
import concourse.bass as bass
import concourse.mybir as mybir
from concourse.bass_utils import run_bass_kernel_spmd

F32 = mybir.dt.float32
BF16 = mybir.dt.bfloat16
AF = mybir.ActivationFunctionType
ALU = mybir.AluOpType

D = 1024
S = 4096
L = 2
DFF = 2816
T = 512
NT = S // T
NCH = 8
NF = 22
EPS = 1e-6
NSLOT = 5
SLOT = 2816

IN_ORDER = []
for _j in range(4):
    IN_ORDER += [_j, 4 + _j]
for _j in range(4):
    IN_ORDER += [12 + _j, 16 + _j]
IN_ORDER += [8, 9, 10, 11]

P_G1 = 0
P_BA = 8
P_CAW = 28
P_CAB = 152
P_LNG = 156
P_LNB = 160
P_CBW = 164
P_G2 = 176
P_CFW = 184
PL = 316
P_GF = 2 * PL
NPRM = 640


def _cols(v):
    v = np.asarray(v, dtype=np.float32).reshape(-1, 128)
    return v.T


def _prep_params(inp):
    cols = []
    for l in range(L):
        cols.append(_cols(inp["mix_norm_g"][l]))
        cols.append(_cols(inp["b_in"][l].reshape(20, 128)[IN_ORDER]))
        caw = inp["conv_a_w"][l].reshape(31, 4, 128).transpose(1, 0, 2)
        cols.append(_cols(caw))
        cols.append(_cols(inp["conv_a_b"][l]))
        cols.append(_cols(inp["ln_a_g"][l]))
        cols.append(_cols(inp["ln_a_b"][l]))
        cbw = inp["conv_b_w"][l].reshape(3, 4, 128).transpose(1, 0, 2)
        cols.append(_cols(cbw))
        cols.append(_cols(inp["ffn_norm_g"][l]))
        cfw = inp["conv_f_w"][l].reshape(3, 2, NF, 128).transpose(2, 1, 0, 3)
        cols.append(_cols(cfw))
    cols.append(_cols(inp["final_norm_g"]))
    prm = np.ascontiguousarray(np.concatenate(cols, axis=1), dtype=np.float32)
    assert prm.shape == (128, NPRM), prm.shape
    return prm


def _prep_weights(inp):
    w_in = np.asarray(inp["w_in"], np.float32)
    w = w_in.reshape(L, 8, 128, 20, 128)[:, :, :, IN_ORDER, :]
    w = w.reshape(L, 8, 128, 10, 2, 128).transpose(0, 3, 2, 1, 4, 5)
    wA = np.ascontiguousarray(w).reshape(L * 1280, 2048)
    w = np.asarray(inp["w_out"], np.float32).reshape(L, 8, 128, 4, 2, 128).transpose(0, 3, 2, 1, 4, 5)
    wO = np.ascontiguousarray(w).reshape(L * 512, 2048)
    w = np.asarray(inp["w_up"], np.float32).reshape(L, 8, 128, 2, NF, 128).transpose(0, 4, 2, 1, 3, 5)
    wU = np.ascontiguousarray(w).reshape(L * NF * 128, 2048)
    w = np.asarray(inp["w_down"], np.float32).reshape(L, NF, 128, 8, 128).transpose(0, 3, 2, 1, 4)
    wD = np.ascontiguousarray(w).reshape(L * 1024, 2816)
    return wA, wO, wU, wD


class Sem:
    def __init__(self, h, name):
        self.h = h
        self.name = name
        self.total = 0


class Eng:
    def __init__(self, h, sem, name):
        self.h = h
        self.sem = sem
        self.name = name
        self.n = 0
        self.seen = {}

    def wait(self, tok):
        if tok is None:
            return
        sem, val = tok
        if self.seen.get(sem.name, 0) >= val:
            return
        self.h.wait_ge(sem.h, val)
        self.seen[sem.name] = val

    def sig(self, ins):
        self.n += 1
        ins.then_inc(self.sem.h, 1)
        return (self.sem, self.n)

    def last(self):
        return (self.sem, self.n) if self.n else None


class Buf:
    def __init__(self):
        self.w = None
        self.r = {}


def acquire(eng, reads=(), writes=()):
    for b in reads:
        eng.wait(b.w)
    for b in writes:
        eng.wait(b.w)
        for t in b.r.values():
            eng.wait(t)


def release(tok, who, reads=(), writes=()):
    for b in reads:
        b.r[who] = tok
    for b in writes:
        b.w = tok
        b.r = {}


def build_program():
    nc = bass.Bass("TRN2", target_bir_lowering=False)
    dt = nc.dram_tensor
    xT = dt("xT", [D, S], F32, kind="ExternalInput").ap()
    wA = dt("wA", [L * 1280, 2048], F32, kind="ExternalInput").ap()
    wO = dt("wO", [L * 512, 2048], F32, kind="ExternalInput").ap()
    wU = dt("wU", [L * NF * 128, 2048], F32, kind="ExternalInput").ap()
    wD = dt("wD", [L * 1024, 2816], F32, kind="ExternalInput").ap()
    prm_d = dt("prm", [128, NPRM], F32, kind="ExternalInput").ap()
    ident_d = dt("ident", [128, 128], F32, kind="ExternalInput").ap()
    outT = dt("outT", [D, S], F32, kind="ExternalOutput").ap()
    sA = dt("sA", [L * 1280, 2048], BF16, kind="Internal").ap()
    sO = dt("sO", [L * 512, 2048], BF16, kind="Internal").ap()
    sU = dt("sU", [L * NF * 128, 2816], BF16, kind="Internal").ap()
    sD = dt("sD", [L * 1024, 2816], BF16, kind="Internal").ap()
    sCA = dt("sCA", [L * 1024, 2048], BF16, kind="Internal").ap()
    sCB = dt("sCB", [L * 128, 1536], BF16, kind="Internal").ap()

    xT_v = xT.rearrange("(c p) t -> p c t", p=128)
    outT_v = outT.rearrange("(c p) t -> p c t", p=128)

    import contextlib
    es = contextlib.ExitStack()
    with es:
        def sb(name, shape, dtype):
            return es.enter_context(nc.sbuf_tensor(name, shape, dtype))

        def mksem(name):
            return Sem(es.enter_context(nc.semaphore(name)), name)

        xs = sb("xs", [128, NT * NCH * T], F32)
        ring = sb("ring", [128, NSLOT * SLOT], BF16)
        hbuf = sb("hbuf", [128, NCH, T], BF16)
        sqb_t = sb("sqb", [128, 2, T], BF16)
        prm = sb("prm_sb", [128, NPRM], F32)
        ident = sb("ident_sb", [128, 128], F32)
        ones1024 = sb("ones1024", [128, 128], BF16)
        ones512 = sb("ones512", [128, 128], BF16)
        epsc = sb("epsc", [128, 1], F32)
        sd_t = sb("sd", [128, T], F32)
        var_t = sb("var", [128, T], F32)
        stg = sb("stg", [128, 17408], BF16)
        ps = es.enter_context(nc.psum_tensor("ps", [128, 8, T], F32))

        def xt(i):
            return xs[:, i * NCH * T:(i + 1) * NCH * T].rearrange("p (c t) -> p c t", c=NCH)

        o = 0
        a_in = stg[:, o:o + 4 * 542].rearrange("p (j t) -> p j t", j=4); o += 4 * 542
        cx = stg[:, o:o + 4 * 514].rearrange("p (j t) -> p j t", j=4); o += 4 * 514
        gb = stg[:, o:o + 4 * T].rearrange("p (j t) -> p j t", j=4); o += 4 * T
        ac_bf = stg[:, o:o + 4 * T].rearrange("p (j t) -> p j t", j=4); o += 4 * T
        asq = stg[:, o:o + 4 * T].rearrange("p (j t) -> p j t", j=4); o += 4 * T
        ybuf = stg[:, o:o + 8 * T].rearrange("p (j t) -> p j t", j=8); o += 8 * T
        tmpM = [stg[:, o + k * 2 * T:o + (k + 1) * 2 * T].bitcast(F32) for k in range(2)]; o += 4 * T
        assert o <= 17408, o
        o = 0
        gbuf = stg[:, o:o + NF * T].rearrange("p (j t) -> p j t", j=NF); o += NF * T
        halu = stg[:, o:o + NF * 8].bitcast(F32).rearrange("p (q t) -> p q t", t=2); o += NF * 8
        contrib = stg[:, o:o + NF * 8].bitcast(F32).rearrange("p (q t) -> p q t", t=2); o += NF * 8
        tmp44 = stg[:, o:o + NF * 4].bitcast(F32); o += NF * 4
        accg = [stg[:, o + k * 2 * T:o + (k + 1) * 2 * T].bitcast(F32) for k in range(3)]; o += 6 * T
        accv = [stg[:, o + k * 2 * T:o + (k + 1) * 2 * T].bitcast(F32) for k in range(2)]; o += 4 * T
        assert o <= 17408, o

        PE = Eng(nc.tensor, mksem("s_pe"), "PE")
        ACT = Eng(nc.scalar, mksem("s_act"), "ACT")
        DVE = Eng(nc.vector, mksem("s_dve"), "DVE")
        POOL = Eng(nc.gpsimd, mksem("s_pool"), "POOL")
        SP = Eng(nc.sync, mksem("s_sp"), "SP")
        engines = [PE, ACT, DVE, POOL, SP]

        def op(eng, fn, reads=(), writes=()):
            acquire(eng, reads, writes)
            tok = eng.sig(fn())
            release(tok, eng.name, reads, writes)
            return tok

        def dma(q, sem, out, in_, reads=(), writes=(), extra=()):
            acquire(q, reads, writes)
            for t in extra:
                q.wait(t)
            q.h.dma_start(out=out, in_=in_).then_inc(sem.h, 16)
            sem.total += 16
            tok = (sem, sem.total)
            release(tok, "dma:" + sem.name, reads, writes)
            return tok

        xb = [Buf() for _ in range(NT)]
        slotb = [Buf() for _ in range(NSLOT)]
        hb = Buf()
        sqbb = [Buf(), Buf()]
        pb = [Buf() for _ in range(8)]
        sdb, rstdb, meanb, varb = Buf(), Buf(), Buf(), Buf()
        ainb = [Buf() for _ in range(4)]
        cxb = [Buf() for _ in range(4)]
        gbb = [Buf() for _ in range(4)]
        acb = [Buf() for _ in range(4)]
        asqb = [Buf() for _ in range(4)]
        yb = [Buf() for _ in range(8)]
        tmpb = [Buf(), Buf()]
        gfb = [Buf() for _ in range(NF)]
        ubm = [Buf() for _ in range(4)]
        ubh = [Buf() for _ in range(4)]
        halb = [Buf() for _ in range(NF)]
        halvb = [Buf() for _ in range(NF)]
        accgb = [Buf(), Buf(), Buf()]
        accvb = [Buf(), Buf()]
        halub = [Buf() for _ in range(2 * NF)]
        contribb = Buf()
        constb = Buf()
        prmb = Buf()
        hbc = [Buf() for _ in range(NCH)]
        pe_pend = []
        state = {"bank": 0, "unit": 0, "tmp": 0}

        def next_tmp():
            k = state["tmp"]
            state["tmp"] = (k + 1) % 2
            return k

        s_slot = [mksem(f"s_slot{k}") for k in range(NSLOT)]

        def next_unit(src, n, deps):
            k = state["unit"] % NSLOT
            state["unit"] += 1
            dst = ring[:, k * SLOT:k * SLOT + n]
            dma(SP, s_slot[k], dst, src, writes=[slotb[k]], extra=deps)
            return ring[:, k * SLOT:(k + 1) * SLOT], slotb[k]

        def mm(b, lhsT, rhs, start, stop, reads, signal):
            acquire(PE, reads, [pb[b]] if start else [])
            ins = nc.tensor.matmul(ps[:, b, :], lhsT=lhsT, rhs=rhs, start=start, stop=stop)
            pe_pend.extend(reads)
            if signal or stop:
                tok = PE.sig(ins)
                release(tok, "PE", list(pe_pend), [pb[b]] if stop else [])
                del pe_pend[:]

        def pcol(c):
            return prm[:, c:c + 1]

        reserved = set()

        def alloc_bank():
            while True:
                b = state["bank"]
                state["bank"] = (b + 1) % 8
                if b not in reserved:
                    return b

        s_c = mksem("s_const")
        s_x = [mksem(f"s_x{i}") for i in range(NT)]
        t_prm = dma(SP, s_c, prm[:], prm_d[:], writes=[prmb])
        t_prm = dma(SP, s_c, ident[:], ident_d[:], writes=[prmb])
        NX0 = 3
        dma(ACT, s_x[0], xt(0), xT_v[:, :, 0:T], writes=[xb[0]])
        op(DVE, lambda: nc.vector.memset(ones1024[:], 1.0 / 1024.0), writes=[constb])
        op(DVE, lambda: nc.vector.memset(ones512[:], 1.0 / 512.0), writes=[constb])
        op(DVE, lambda: nc.vector.memset(epsc[:], EPS), writes=[constb])

        cast_tokens = {}

        def cast_piece(name, l, u0, u1):
            sem = mksem(f"s_cast_{name}{l}_{u0}")
            if name == "A":
                r0, r1 = (l * 10 + u0) * 128, (l * 10 + u1) * 128
                dst, src = sA[r0:r1, :], wA[r0:r1, :]
            elif name == "O":
                r0, r1 = (l * 4 + u0) * 128, (l * 4 + u1) * 128
                dst, src = sO[r0:r1, :], wO[r0:r1, :]
            elif name == "U":
                r0, r1 = (l * NF + u0) * 128, (l * NF + u1) * 128
                dst, src = sU[r0:r1, 0:2048], wU[r0:r1, :]
            else:
                r0, r1 = (l * 8 + u0) * 128, (l * 8 + u1) * 128
                dst = sD[r0:r1, :].rearrange("r (two h) -> (r two) h", two=2)
                src = wD[r0:r1, :].rearrange("r (two h) -> (r two) h", two=2)
            POOL.h.dma_start(out=dst, in_=src).then_inc(sem.h, 16)
            for u in range(u0, u1):
                cast_tokens[(name, l, u)] = (sem, 16)

        def pieces_of(name, l, n):
            return [(name, l, a, min(a + 2, n)) for a in range(0, n, 2)]

        cast_sched = {}
        pieces0 = pieces_of("U", 0, NF) + pieces_of("D", 0, 8) + pieces_of("A", 1, 10) + pieces_of("O", 1, 4)
        pieces1 = pieces_of("U", 1, NF) + pieces_of("D", 1, 8)
        for si_, pcs in ((0, pieces0), (2, pieces1)):
            per = -(-len(pcs) // (NT * 3))
            k = 0
            for tl in range(NT):
                for pt in range(3):
                    cast_sched[(si_, tl, pt)] = pcs[k:k + per]
                    k += per
            assert k >= len(pcs)

        cast_piece("A", 0, 0, 2)

        def prologue_casts_rest():
            POOL.wait((s_x[0], 16))
            for a in range(2, 10, 2):
                cast_piece("A", 0, a, a + 2)
            dma(POOL, s_x[1], xt(1), xT_v[:, :, T:2 * T], writes=[xb[1]])
            cast_piece("O", 0, 0, 2)
            cast_piece("O", 0, 2, 4)
            dma(POOL, s_x[2], xt(2), xT_v[:, :, 2 * T:3 * T], writes=[xb[2]])

        XO = NX0 * NCH * T
        R1 = xs[:, XO:XO + 8192].bitcast(BF16)
        RBr = xs[:, XO + 8192:XO + 8192 + 768].bitcast(BF16)
        RFr = xs[:, XO + 8192 + 768:XO + 8192 + 768 + 8448].bitcast(BF16)
        r1b, rbb, rfb = Buf(), Buf(), Buf()
        r1a = Buf()
        RA = R1.rearrange("p (j k n) -> p j k n", j=4, k=32)
        RB = RBr.rearrange("p (k n) -> p k n", k=12)
        RF = RFr.rearrange("p (k n) -> p k n", k=132)
        s_dg = {}

        def dg_tok(name, l):
            sem = s_dg[(name, l)]
            return (sem, sem.total)

        def bcast_build(eng, out, c0, n):
            return eng.h.tensor_tensor(out=out, in0=ident[:].unsqueeze(1).to_broadcast([128, n, 128]),
                                       in1=prm[:, c0:c0 + n].unsqueeze(2).to_broadcast([128, n, 128]), op=ALU.mult)

        def build_A(eng, l, reg, regb, js=(0, 1, 2, 3), store=True, regb2=None):
            ra = reg.rearrange("p (j k n) -> p j k n", j=4, k=32)
            acquire(eng, [constb, prmb], [regb])
            ins = None
            for j in js:
                bcast_build(eng, ra[:, j, 0:31, :], l * PL + P_CAW + j * 31, 31)
                ins = eng.h.memset(ra[:, j, 31, :], 0.0)
            release(eng.sig(ins), eng.name, [constb, prmb], [regb])
            if not store:
                return
            if regb2 is not None:
                POOL.wait(regb2.w)
            sem = mksem(f"s_dgA{l}")
            s_dg[("A", l)] = sem
            dma(POOL, sem, sCA[l * 1024:(l + 1) * 1024, :].rearrange("(u p) n -> p u n", p=128),
                reg.rearrange("p (u n) -> p u n", u=8), reads=[regb])

        def build_B(eng, l):
            acquire(eng, [constb, prmb], [rbb])
            ins = bcast_build(eng, RB, l * PL + P_CBW, 12)
            release(eng.sig(ins), eng.name, [constb, prmb], [rbb])
            sem = mksem(f"s_dgB{l}")
            s_dg[("B", l)] = sem
            dma(POOL, sem, sCB[l * 128:(l + 1) * 128, :], RBr, reads=[rbb])

        def build_F(eng, l):
            acquire(eng, [constb, prmb], [rfb])
            ins = bcast_build(eng, RF, l * PL + P_CFW, 132)
            release(eng.sig(ins), eng.name, [constb, prmb], [rfb])
            sem = mksem(f"s_dgF{l}")
            s_dg[("F", l)] = sem
            r0, r1 = l * NF * 128, (l + 1) * NF * 128
            dma(POOL, sem, sU[r0:r1, 2048:2816].rearrange("(j p) n -> p j n", p=128),
                RFr.rearrange("p (j n) -> p j n", j=NF), reads=[rfb])

        def prologue_builds():
            build_A(DVE, 0, R1, r1a, js=(0, 1), store=False)
            build_B(DVE, 0)
            build_A(POOL, 0, R1, r1b, js=(2, 3), store=True, regb2=r1a)
            prologue_casts_rest()
            for i in (3, 4):
                dma(POOL, s_x[i], xt(i), xT_v[:, :, i * T:(i + 1) * T], writes=[xb[i], r1b, r1a])
            build_A(POOL, 1, RFr[:, 0:16384], rfb)
            build_B(POOL, 1)
            for i in (5, 6, 7):
                dma(POOL, s_x[i], xt(i), xT_v[:, :, i * T:(i + 1) * T], writes=[xb[i], rbb, rfb])

        pend = {}
        late_unreserve = []

        def flush_unreserve():
            while late_unreserve:
                reserved.discard(late_unreserve.pop())

        def norm_p1_steps(key, i):
            x_i = xt(i)

            def step(k):
                if k < NCH:
                    q = k % 2
                    op(ACT, lambda: nc.scalar.activation(out=sqb_t[:, q, :], in_=x_i[:, k, :], func=AF.Square),
                       reads=[xb[i]], writes=[sqbb[q]])
                if k >= 1:
                    c = k - 1
                    if c == 0:
                        b = alloc_bank()
                        reserved.add(b)
                        pend[key] = b
                    b = pend[key]
                    q = c % 2
                    mm(b, ones1024[:], sqb_t[:, q, :], c == 0, c == NCH - 1, [sqbb[q], constb, prmb], True)
            return [(lambda k=k: step(k)) for k in range(NCH + 1)]

        def norm_p2(key, i, gcol, out_fn, out_bufs_w):
            b = pend.pop(key)
            x_i = xt(i)
            op(ACT, lambda: nc.scalar.activation(out=sd_t[:], in_=ps[:, b, :], func=AF.Sqrt, bias=epsc[:, 0:1], scale=1.0),
               reads=[pb[b], constb, prmb], writes=[sdb])
            op(DVE, lambda: nc.vector.reciprocal(out=ps[:, b, :], in_=sd_t[:]), reads=[sdb], writes=[pb[b]])
            for c in range(NCH):
                op(DVE, lambda: nc.vector.scalar_tensor_tensor(out=out_fn(c), in0=x_i[:, c, :], scalar=pcol(gcol + c),
                                                               in1=ps[:, b, :], op0=ALU.mult, op1=ALU.mult),
                   reads=[xb[i], pb[b], constb, prmb], writes=out_bufs_w(c))
            late_unreserve.append(b)

        def h_out(c):
            return hbuf[:, c, :]

        def issue_casts(si, i, pt):
            pcs = cast_sched.get((si, i, pt), [])
            if pcs:
                POOL.wait(PE.last())
                for pc in pcs:
                    cast_piece(*pc)

        def mixer_tile(l, i, steps, p2s):
            base = l * PL
            steps = list(steps)

            def inproj_unit(u):
                slot, sbf = next_unit(sA[(l * 10 + u) * 128:(l * 10 + u + 1) * 128, :], 2048, [cast_tokens[("A", l, u)]])
                W = slot[:, 0:2048].rearrange("p (c n) -> p c n", c=NCH)
                banks = []
                for s in range(2):
                    b = alloc_bank()
                    for c in range(NCH):
                        mm(b, W[:, c, s * 128:(s + 1) * 128], hbuf[:, c, :], c == 0, c == NCH - 1, [sbf, hbc[c]], False)
                    banks.append(b)
                bA, bB = banks
                c0 = base + P_BA + 2 * u
                if u < 8:
                    j = u % 4
                    k = next_tmp()
                    fn = AF.Sigmoid if u < 4 else AF.Identity
                    op(ACT, lambda: nc.scalar.activation(out=tmpM[k][:], in_=ps[:, bB, :], func=fn, bias=pcol(c0 + 1), scale=1.0),
                       reads=[pb[bB], constb, prmb], writes=[tmpb[k]])
                    if u < 4:
                        dst, dbuf = a_in[:, j, 30:30 + T], ainb[j]
                    else:
                        dst, dbuf = cx[:, j, 2:2 + T], cxb[j]
                    op(DVE, lambda: nc.vector.scalar_tensor_tensor(out=dst, in0=ps[:, bA, :], scalar=pcol(c0), in1=tmpM[k][:],
                                                                   op0=ALU.add, op1=ALU.mult),
                       reads=[pb[bA], tmpb[k], constb, prmb], writes=[dbuf])
                else:
                    for s in range(2):
                        j = (u - 8) * 2 + s
                        bb = banks[s]
                        op(ACT, lambda: nc.scalar.activation(out=gb[:, j, :], in_=ps[:, bb, :], func=AF.Identity, bias=pcol(c0 + s), scale=1.0),
                           reads=[pb[bb], constb, prmb], writes=[gbb[j]])

            for u in range(4):
                inproj_unit(u)
            for j in range(4):
                b = alloc_bank()
                for half in range(2):
                    uu = l * 8 + j * 2 + half
                    slot, sbf = next_unit(sCA[uu * 128:(uu + 1) * 128, :], 2048, [dg_tok("A", l)])
                    ntap = 16 if half == 0 else 15
                    for kk in range(ntap):
                        kt = half * 16 + kk
                        mm(b, slot[:, kk * 128:(kk + 1) * 128], a_in[:, j, kt:kt + T], kt == 0, kt == 30,
                           [sbf, ainb[j]], kk == ntap - 1)
                        if steps and kk in (4, 9, 14):
                            steps.pop(0)()
                cb = base + P_CAB + j
                op(ACT, lambda: nc.scalar.activation(out=ac_bf[:, j, :], in_=ps[:, b, :], func=AF.Identity, bias=pcol(cb), scale=1.0),
                   reads=[pb[b], constb, prmb], writes=[acb[j]])
                op(ACT, lambda: nc.scalar.activation(out=asq[:, j, :], in_=ps[:, b, :], func=AF.Square, bias=pcol(cb), scale=1.0),
                   reads=[pb[b], constb, prmb], writes=[asqb[j]])
            while steps:
                steps.pop(0)()
            if i + 1 < NT:
                op(DVE, lambda: nc.vector.tensor_copy(out=a_in[:, :, 0:30], in_=a_in[:, :, T:T + 30]), reads=[], writes=ainb)
            issue_casts(cur[0][0], cur[0][1], 1)
            bm = alloc_bank()
            reserved.add(bm)
            bq = alloc_bank()
            reserved.add(bq)
            for j in range(4):
                mm(bm, ones512[:], ac_bf[:, j, :], j == 0, j == 3, [acb[j], constb, prmb], False)
                mm(bq, ones512[:], asq[:, j, :], j == 0, j == 3, [asqb[j], constb, prmb], False)

            def ln_head():
                op(ACT, lambda: nc.scalar.activation(out=var_t[:], in_=ps[:, bm, :], func=AF.Square), reads=[pb[bm]], writes=[varb])
                op(DVE, lambda: nc.vector.tensor_tensor(out=var_t[:], in0=ps[:, bq, :], in1=var_t[:], op=ALU.subtract),
                   reads=[pb[bq], varb], writes=[varb])
                op(ACT, lambda: nc.scalar.activation(out=sd_t[:], in_=var_t[:], func=AF.Sqrt, bias=epsc[:, 0:1], scale=1.0),
                   reads=[varb, constb, prmb], writes=[sdb])
                op(DVE, lambda: nc.vector.reciprocal(out=ps[:, bq, :], in_=sd_t[:]), reads=[sdb], writes=[pb[bq]])

            def ln_j(j):
                k = next_tmp()
                op(DVE, lambda: nc.vector.tensor_tensor(out=tmpM[k][:], in0=ac_bf[:, j, :], in1=ps[:, bm, :], op=ALU.subtract),
                   reads=[acb[j], pb[bm]], writes=[tmpb[k]])
                op(DVE, lambda: nc.vector.tensor_tensor(out=tmpM[k][:], in0=tmpM[k][:], in1=ps[:, bq, :], op=ALU.mult),
                   reads=[tmpb[k], pb[bq]], writes=[tmpb[k]])
                op(ACT, lambda: nc.scalar.activation(out=ybuf[:, j, :], in_=tmpM[k][:], func=AF.Silu,
                                                     bias=pcol(base + P_LNB + j), scale=pcol(base + P_LNG + j)),
                   reads=[tmpb[k], constb, prmb], writes=[yb[j]])

            for u in range(4, 10):
                inproj_unit(u)
                if u == 4:
                    ln_head()
                elif u <= 8:
                    ln_j(u - 5)
            reserved.discard(bm)
            reserved.discard(bq)
            issue_casts(cur[0][0], cur[0][1], 2)
            slot, sbf = next_unit(sCB[l * 128:(l + 1) * 128, :], 1536, [dg_tok("B", l)])
            for j in range(4):
                b = alloc_bank()
                for kt in range(3):
                    mm(b, slot[:, (j * 3 + kt) * 128:(j * 3 + kt + 1) * 128], cx[:, j, kt:kt + T], kt == 0, kt == 2,
                       [sbf, cxb[j]], False)
                op(DVE, lambda: nc.vector.tensor_tensor(out=ybuf[:, 4 + j, :], in0=ps[:, b, :], in1=gb[:, j, :], op=ALU.mult),
                   reads=[pb[b], gbb[j]], writes=[yb[4 + j]])
            if i + 1 < NT:
                op(ACT, lambda: nc.scalar.activation(out=cx[:, :, 0:2], in_=cx[:, :, T:T + 2], func=AF.Identity), reads=[], writes=cxb)
            for f in p2s:
                f()
            x_i = xt(i)
            for u in range(4):
                slot, sbf = next_unit(sO[(l * 4 + u) * 128:(l * 4 + u + 1) * 128, :], 2048, [cast_tokens[("O", l, u)]])
                W = slot[:, 0:2048].rearrange("p (c n) -> p c n", c=NCH)
                for s in range(2):
                    oc = 2 * u + s
                    b = alloc_bank()
                    for c in range(NCH):
                        mm(b, W[:, c, s * 128:(s + 1) * 128], ybuf[:, c, :], c == 0, c == NCH - 1, [sbf, yb[c]], False)
                    op(DVE, lambda: nc.vector.tensor_tensor(out=x_i[:, oc, :], in0=ps[:, b, :], in1=x_i[:, oc, :], op=ALU.add),
                       reads=[pb[b]], writes=[xb[i]])

        def ffn_tile(l, i, steps, p2s, tail):
            units = {}
            steps = list(steps)

            cf = prm[:, l * PL + P_CFW:l * PL + P_CFW + 6 * NF].rearrange("p (q k) -> p q k", k=3)
            op(DVE, lambda: nc.vector.tensor_tensor(out=contrib[:, :, 1], in0=halu[:, :, 1], in1=cf[:, :, 0], op=ALU.mult),
               reads=halub + [prmb], writes=[contribb])
            op(DVE, lambda: nc.vector.tensor_tensor(out=contrib[:, :, 0], in0=halu[:, :, 0], in1=cf[:, :, 0], op=ALU.mult),
               reads=halub + [prmb], writes=[contribb])
            op(DVE, lambda: nc.vector.tensor_tensor(out=tmp44, in0=halu[:, :, 1], in1=cf[:, :, 1], op=ALU.mult),
               reads=halub + [prmb], writes=[contribb])
            op(DVE, lambda: nc.vector.tensor_tensor(out=contrib[:, :, 0], in0=contrib[:, :, 0], in1=tmp44, op=ALU.add),
               reads=[contribb], writes=[contribb])

            def conv_taps(b, acc, accbuf, q, c0):
                op(ACT, lambda: nc.scalar.activation(out=acc[:, 0:T], in_=ps[:, b, :], func=AF.Identity, scale=pcol(c0 + 2)),
                   reads=[pb[b], prmb], writes=[accbuf])
                op(ACT, lambda: nc.scalar.activation(out=halu[:, q, :], in_=ps[:, b, T - 2:T], func=AF.Identity),
                   reads=[pb[b]], writes=[halub[q]])
                op(DVE, lambda: nc.vector.scalar_tensor_tensor(out=acc[:, 1:T], in0=ps[:, b, 0:T - 1], scalar=pcol(c0 + 1),
                                                               in1=acc[:, 1:T], op0=ALU.mult, op1=ALU.add),
                   reads=[pb[b], accbuf, prmb], writes=[accbuf])
                op(DVE, lambda: nc.vector.scalar_tensor_tensor(out=acc[:, 2:T], in0=ps[:, b, 0:T - 2], scalar=pcol(c0),
                                                               in1=acc[:, 2:T], op0=ALU.mult, op1=ALU.add),
                   reads=[pb[b], accbuf, prmb], writes=[accbuf])
                op(POOL, lambda: nc.gpsimd.tensor_tensor(out=acc[:, 0:2], in0=acc[:, 0:2], in1=contrib[:, q, :], op=ALU.add),
                   reads=[contribb, accbuf], writes=[accbuf])

            def up(j):
                slot, sbf = next_unit(sU[(l * NF + j) * 128:(l * NF + j + 1) * 128, 0:2048], 2048, [cast_tokens[("U", l, j)]])
                W = slot[:, 0:2048].rearrange("p (c n) -> p c n", c=NCH)
                bks = []
                for s_ in range(2):
                    b = alloc_bank()
                    for c in range(NCH):
                        mm(b, W[:, c, s_ * 128:(s_ + 1) * 128], hbuf[:, c, :], c == 0, c == NCH - 1, [sbf, hbc[c]], False)
                    bks.append(b)
                bG, bV = bks
                kg, kv = j % 3, j % 2
                c0 = l * PL + P_CFW + j * 6
                conv_taps(bG, accg[kg], accgb[kg], 2 * j, c0)
                conv_taps(bV, accv[kv], accvb[kv], 2 * j + 1, c0 + 3)
                units[j] = (kg, kv)

            def conv(j):
                kg, kv = units.pop(j)
                op(ACT, lambda: nc.scalar.activation(out=accg[kg][:], in_=accg[kg][:], func=AF.Silu),
                   reads=[accgb[kg]], writes=[accgb[kg]])
                op(POOL, lambda: nc.gpsimd.tensor_tensor(out=gbuf[:, j, :], in0=accv[kv][:], in1=accg[kg][:], op=ALU.mult),
                   reads=[accvb[kv], accgb[kg]], writes=[gfb[j]])

            for j in range(NF + 1):
                if j < NF:
                    up(j)
                if j >= 1:
                    conv(j - 1)
                if steps and j >= 2:
                    steps.pop(0)()
            while steps:
                steps.pop(0)()
            for f in p2s:
                f()
            x_i = xt(i)
            for oc in range(NCH):
                slot, sbf = next_unit(sD[(l * 8 + oc) * 128:(l * 8 + oc + 1) * 128, :], 2816, [cast_tokens[("D", l, oc)]])
                b = alloc_bank()
                for fc in range(NF):
                    mm(b, slot[:, fc * 128:(fc + 1) * 128], gbuf[:, fc, :], fc == 0, fc == NF - 1, [sbf, gfb[fc]], False)
                op(DVE, lambda: nc.vector.tensor_tensor(out=x_i[:, oc, :], in0=ps[:, b, :], in1=x_i[:, oc, :], op=ALU.add),
                   reads=[pb[b]], writes=[xb[i]])
            for f in tail:
                f()

        def barrier():
            toks = [e.last() for e in engines]
            for e in engines:
                for t in toks:
                    if t is not None and t[0] is not e.sem:
                        e.wait(t)

        stages = []
        for l in range(L):
            stages.append(("M", l, l * PL + P_G1))
            stages.append(("F", l, l * PL + P_G2))

        s_out = mksem("s_out")

        def final_p2(i):
            x_i = xt(i)
            norm_p2(("fin", i), i, P_GF, lambda c: x_i[:, c, :], lambda c: [xb[i]])

        def final_store(i):
            dma(SP, s_out, outT_v[:, :, i * T:(i + 1) * T], xt(i), reads=[xb[i]])

        prologue_builds()
        for f in norm_p1_steps((0, 0), 0):
            f()
        norm_p2((0, 0), 0, stages[0][2], h_out, lambda c: [hbc[c]])
        flush_unreserve()
        last_si = len(stages) - 1
        cur = [None]
        for si, (kind, l, gcol) in enumerate(stages):
            if kind == "M":
                op(DVE, lambda: nc.vector.memset(a_in[:, :, 0:30], 0.0), writes=ainb)
                op(DVE, lambda: nc.vector.memset(cx[:, :, 0:2], 0.0), writes=cxb)
            else:
                op(DVE, lambda: nc.vector.memset(halu, 0.0), writes=halub)
            for i in range(NT):
                cur[0] = (si, i)
                issue_casts(si, i, 0)
                steps, p2s, tail = [], [], []
                if i + 1 < NT:
                    nk, ni, ng = (si, i + 1), i + 1, gcol
                elif si + 1 < len(stages):
                    nk, ni, ng = (si + 1, 0), 0, stages[si + 1][2]
                else:
                    nk = None
                if nk is not None:
                    steps += norm_p1_steps(nk, ni)
                    p2s.append(lambda nk=nk, ni=ni, ng=ng: norm_p2(nk, ni, ng, h_out, lambda c: [hbc[c]]))
                if si == last_si and i >= 1:
                    steps += norm_p1_steps(("fin", i - 1), i - 1)
                    p2s.append(lambda i=i: final_p2(i - 1))
                    tail.append(lambda i=i: final_store(i - 1))
                if kind == "M":
                    mixer_tile(l, i, steps, p2s)
                else:
                    ffn_tile(l, i, steps, p2s, tail)
                flush_unreserve()
            barrier()
        for f in norm_p1_steps(("fin", NT - 1), NT - 1):
            f()
        final_p2(NT - 1)
        final_store(NT - 1)
        SP.wait((s_out, s_out.total))
    return nc


_CACHE = {}


def kernel(**inputs):
    x = np.asarray(inputs["x"], np.float32)
    B = x.shape[0]
    assert x.shape == (8, S, D)
    prm = _prep_params({k: np.asarray(v, np.float32) for k, v in inputs.items() if k not in ("x",)})
    wA, wO, wU, wD = _prep_weights(inputs)
    ident = np.eye(128, dtype=np.float32)
    if "nc" not in _CACHE:
        _CACHE["nc"] = build_program()
    nc = _CACHE["nc"]
    in_maps = []
    for b in range(B):
        in_maps.append({"xT": np.ascontiguousarray(x[b].T), "wA": wA, "wO": wO, "wU": wU, "wD": wD,
                        "prm": prm, "ident": ident})
    res = run_bass_kernel_spmd(nc, in_maps, core_ids=list(range(B)))
    out = np.stack([np.ascontiguousarray(res.results[b]["outT"].T) for b in range(B)], axis=0)
    return out.astype(np.float32)
```

```python
import numpy as np
import concourse.bass as bass
import concourse.mybir as mybir
from concourse.bass_utils import run_bass_kernel_spmd

F32 = mybir.dt.float32
BF16 = mybir.dt.bfloat16
AF = mybir.ActivationFunctionType
ALU = mybir.AluOpType

D = 1024
S = 4096
L = 2
DFF = 2816
T = 512
NT = S // T
NCH = 8
NF = 22
EPS = 1e-6
NSLOT = 5
SLOT = 2816

IN_ORDER = []
for _j in range(4):
    IN_ORDER += [_j, 4 + _j]
for _j in range(4):
    IN_ORDER += [12 + _j, 16 + _j]
IN_ORDER += [8, 9, 10, 11]

P_G1 = 0
P_BA = 8
P_CAW = 28
P_CAB = 152
P_LNG = 156
P_LNB = 160
P_CBW = 164
P_G2 = 176
P_CFW = 184
PL = 316
P_GF = 2 * PL
NPRM = 640


def _cols(v):
    v = np.asarray(v, dtype=np.float32).reshape(-1, 128)
    return v.T


def _prep_params(inp):
    cols = []
    for l in range(L):
        cols.append(_cols(inp["mix_norm_g"][l]))
        cols.append(_cols(inp["b_in"][l].reshape(20, 128)[IN_ORDER]))
        caw = inp["conv_a_w"][l].reshape(31, 4, 128).transpose(1, 0, 2)
        cols.append(_cols(caw))
        cols.append(_cols(inp["conv_a_b"][l]))
        cols.append(_cols(inp["ln_a_g"][l]))
        cols.append(_cols(inp["ln_a_b"][l]))
        cbw = inp["conv_b_w"][l].reshape(3, 4, 128).transpose(1, 0, 2)
        cols.append(_cols(cbw))
        cols.append(_cols(inp["ffn_norm_g"][l]))
        cfw = inp["conv_f_w"][l].reshape(3, 2, NF, 128).transpose(2, 1, 0, 3)
        cols.append(_cols(cfw))
    cols.append(_cols(inp["final_norm_g"]))
    prm = np.ascontiguousarray(np.concatenate(cols, axis=1), dtype=np.float32)
    assert prm.shape == (128, NPRM), prm.shape
    return prm


def _prep_weights(inp):
    w_in = np.asarray(inp["w_in"], np.float32)
    w = w_in.reshape(L, 8, 128, 20, 128)[:, :, :, IN_ORDER, :]
    w = w.reshape(L, 8, 128, 10, 2, 128).transpose(0, 3, 2, 1, 4, 5)
    wA = np.ascontiguousarray(w).reshape(L * 1280, 2048)
    w = np.asarray(inp["w_out"], np.float32).reshape(L, 8, 128, 4, 2, 128).transpose(0, 3, 2, 1, 4, 5)
    wO = np.ascontiguousarray(w).reshape(L * 512, 2048)
    w = np.asarray(inp["w_up"], np.float32).reshape(L, 8, 128, 2, NF, 128).transpose(0, 4, 2, 1, 3, 5)
    wU = np.ascontiguousarray(w).reshape(L * NF * 128, 2048)
    w = np.asarray(inp["w_down"], np.float32).reshape(L, NF, 128, 8, 128).transpose(0, 3, 2, 1, 4)
    wD = np.ascontiguousarray(w).reshape(L * 1024, 2816)
    return wA, wO, wU, wD


class Sem:
    def __init__(self, h, name):
        self.h = h
        self.name = name
        self.total = 0


class Eng:
    def __init__(self, h, sem, name):
        self.h = h
        self.sem = sem
        self.name = name
        self.n = 0
        self.seen = {}

    def wait(self, tok):
        if tok is None:
            return
        sem, val = tok
        if self.seen.get(sem.name, 0) >= val:
            return
        self.h.wait_ge(sem.h, val)
        self.seen[sem.name] = val

    def sig(self, ins):
        self.n += 1
        ins.then_inc(self.sem.h, 1)
        return (self.sem, self.n)

    def last(self):
        return (self.sem, self.n) if self.n else None


class Buf:
    def __init__(self):
        self.w = None
        self.r = {}


def acquire(eng, reads=(), writes=()):
    for b in reads:
        eng.wait(b.w)
    for b in writes:
        eng.wait(b.w)
        for t in b.r.values():
            eng.wait(t)


def release(tok, who, reads=(), writes=()):
    for b in reads:
        b.r[who] = tok
    for b in writes:
        b.w = tok
        b.r = {}


def build_program():
    nc = bass.Bass("TRN2", target_bir_lowering=False)
    dt = nc.dram_tensor
    xT = dt("xT", [D, S], F32, kind="ExternalInput").ap()
    wA = dt("wA", [L * 1280, 2048], F32, kind="ExternalInput").ap()
    wO = dt("wO", [L * 512, 2048], F32, kind="ExternalInput").ap()
    wU = dt("wU", [L * NF * 128, 2048], F32, kind="ExternalInput").ap()
    wD = dt("wD", [L * 1024, 2816], F32, kind="ExternalInput").ap()
    prm_d = dt("prm", [128, NPRM], F32, kind="ExternalInput").ap()
    ident_d = dt("ident", [128, 128], F32, kind="ExternalInput").ap()
    outT = dt("outT", [D, S], F32, kind="ExternalOutput").ap()
    sA = dt("sA", [L * 1280, 2048], BF16, kind="Internal").ap()
    sO = dt("sO", [L * 512, 2048], BF16, kind="Internal").ap()
    sU = dt("sU", [L * NF * 128, 2816], BF16, kind="Internal").ap()
    sD = dt("sD", [L * 1024, 2816], BF16, kind="Internal").ap()
    sCA = dt("sCA", [L * 1024, 2048], BF16, kind="Internal").ap()
    sCB = dt("sCB", [L * 128, 1536], BF16, kind="Internal").ap()

    xT_v = xT.rearrange("(c p) t -> p c t", p=128)
    outT_v = outT.rearrange("(c p) t -> p c t", p=128)

    import contextlib
    es = contextlib.ExitStack()
    with es:
        def sb(name, shape, dtype):
            return es.enter_context(nc.sbuf_tensor(name, shape, dtype))

        def mksem(name):
            return Sem(es.enter_context(nc.semaphore(name)), name)

        xs = sb("xs", [128, NT * NCH * T], F32)
        ring = sb("ring", [128, NSLOT * SLOT], BF16)
        hbuf = sb("hbuf", [128, NCH, T], BF16)
        sqb_t = sb("sqb", [128, 2, T], BF16)
        prm = sb("prm_sb", [128, NPRM], F32)
        ident = sb("ident_sb", [128, 128], F32)
        ones1024 = sb("ones1024", [128, 128], BF16)
        ones512 = sb("ones512", [128, 128], BF16)
        epsc = sb("epsc", [128, 1], F32)
        sd_t = sb("sd", [128, T], F32)
        var_t = sb("var", [128, T], F32)
        stg = sb("stg", [128, 17408], BF16)
        ps = es.enter_context(nc.psum_tensor("ps", [128, 8, T], F32))

        def xt(i):
            return xs[:, i * NCH * T:(i + 1) * NCH * T].rearrange("p (c t) -> p c t", c=NCH)

        o = 0
        a_in = stg[:, o:o + 4 * 542].rearrange("p (j t) -> p j t", j=4); o += 4 * 542
        cx = stg[:, o:o + 4 * 514].rearrange("p (j t) -> p j t", j=4); o += 4 * 514
        gb = stg[:, o:o + 4 * T].rearrange("p (j t) -> p j t", j=4); o += 4 * T
        ac_bf = stg[:, o:o + 4 * T].rearrange("p (j t) -> p j t", j=4); o += 4 * T
        asq = stg[:, o:o + 4 * T].rearrange("p (j t) -> p j t", j=4); o += 4 * T
        ybuf = stg[:, o:o + 8 * T].rearrange("p (j t) -> p j t", j=8); o += 8 * T
        tmpM = [stg[:, o + k * 2 * T:o + (k + 1) * 2 * T].bitcast(F32) for k in range(2)]; o += 4 * T
        assert o <= 17408, o
        o = 0
        gbuf = stg[:, o:o + NF * T].rearrange("p (j t) -> p j t", j=NF); o += NF * T
        halu = stg[:, o:o + NF * 8].bitcast(F32).rearrange("p (q t) -> p q t", t=2); o += NF * 8
        contrib = stg[:, o:o + NF * 8].bitcast(F32).rearrange("p (q t) -> p q t", t=2); o += NF * 8
        tmp44 = stg[:, o:o + NF * 4].bitcast(F32); o += NF * 4
        accg = [stg[:, o + k * 2 * T:o + (k + 1) * 2 * T].bitcast(F32) for k in range(3)]; o += 6 * T
        accv = [stg[:, o + k * 2 * T:o + (k + 1) * 2 * T].bitcast(F32) for k in range(2)]; o += 4 * T
        assert o <= 17408, o

        PE = Eng(nc.tensor, mksem("s_pe"), "PE")
        ACT = Eng(nc.scalar, mksem("s_act"), "ACT")
        DVE = Eng(nc.vector, mksem("s_dve"), "DVE")
        POOL = Eng(nc.gpsimd, mksem("s_pool"), "POOL")
        SP = Eng(nc.sync, mksem("s_sp"), "SP")
        engines = [PE, ACT, DVE, POOL, SP]

        def op(eng, fn, reads=(), writes=()):
            acquire(eng, reads, writes)
            tok = eng.sig(fn())
            release(tok, eng.name, reads, writes)
            return tok

        def dma(q, sem, out, in_, reads=(), writes=(), extra=()):
            acquire(q, reads, writes)
            for t in extra:
                q.wait(t)
            q.h.dma_start(out=out, in_=in_).then_inc(sem.h, 16)
            sem.total += 16
            tok = (sem, sem.total)
            release(tok, "dma:" + sem.name, reads, writes)
            return tok

        xb = [Buf() for _ in range(NT)]
        slotb = [Buf() for _ in range(NSLOT)]
        hb = Buf()
        sqbb = [Buf(), Buf()]
        pb = [Buf() for _ in range(8)]
        sdb, rstdb, meanb, varb = Buf(), Buf(), Buf(), Buf()
        ainb = [Buf() for _ in range(4)]
        cxb = [Buf() for _ in range(4)]
        gbb = [Buf() for _ in range(4)]
        acb = [Buf() for _ in range(4)]
        asqb = [Buf() for _ in range(4)]
        yb = [Buf() for _ in range(8)]
        tmpb = [Buf(), Buf()]
        gfb = [Buf() for _ in range(NF)]
        ubm = [Buf() for _ in range(4)]
        ubh = [Buf() for _ in range(4)]
        halb = [Buf() for _ in range(NF)]
        halvb = [Buf() for _ in range(NF)]
        accgb = [Buf(), Buf(), Buf()]
        accvb = [Buf(), Buf()]
        halub = [Buf() for _ in range(2 * NF)]
        contribb = Buf()
        constb = Buf()
        prmb = Buf()
        hbc = [Buf() for _ in range(NCH)]
        pe_pend = []
        state = {"bank": 0, "unit": 0, "tmp": 0}

        def next_tmp():
            k = state["tmp"]
            state["tmp"] = (k + 1) % 2
            return k

        s_slot = [mksem(f"s_slot{k}") for k in range(NSLOT)]

        def next_unit(src, n, deps):
            k = state["unit"] % NSLOT
            state["unit"] += 1
            dst = ring[:, k * SLOT:k * SLOT + n]
            dma(SP, s_slot[k], dst, src, writes=[slotb[k]], extra=deps)
            return ring[:, k * SLOT:(k + 1) * SLOT], slotb[k]

        def mm(b, lhsT, rhs, start, stop, reads, signal):
            acquire(PE, reads, [pb[b]] if start else [])
            ins = nc.tensor.matmul(ps[:, b, :], lhsT=lhsT, rhs=rhs, start=start, stop=stop)
            pe_pend.extend(reads)
            if signal or stop:
                tok = PE.sig(ins)
                release(tok, "PE", list(pe_pend), [pb[b]] if stop else [])
                del pe_pend[:]

        def pcol(c):
            return prm[:, c:c + 1]

        reserved = set()

        def alloc_bank():
            while True:
                b = state["bank"]
                state["bank"] = (b + 1) % 8
                if b not in reserved:
                    return b

        s_c = mksem("s_const")
        s_x = [mksem(f"s_x{i}") for i in range(NT)]
        t_prm = dma(SP, s_c, prm[:], prm_d[:], writes=[prmb])
        t_prm = dma(SP, s_c, ident[:], ident_d[:], writes=[prmb])
        NX0 = 3
        dma(ACT, s_x[0], xt(0), xT_v[:, :, 0:T], writes=[xb[0]])
        op(DVE, lambda: nc.vector.memset(ones1024[:], 1.0 / 1024.0), writes=[constb])
        op(DVE, lambda: nc.vector.memset(ones512[:], 1.0 / 512.0), writes=[constb])
        op(DVE, lambda: nc.vector.memset(epsc[:], EPS), writes=[constb])

        cast_tokens = {}

        def cast_piece(name, l, u0, u1):
            sem = mksem(f"s_cast_{name}{l}_{u0}")
            if name == "A":
                r0, r1 = (l * 10 + u0) * 128, (l * 10 + u1) * 128
                dst, src = sA[r0:r1, :], wA[r0:r1, :]
            elif name == "O":
                r0, r1 = (l * 4 + u0) * 128, (l * 4 + u1) * 128
                dst, src = sO[r0:r1, :], wO[r0:r1, :]
            elif name == "U":
                r0, r1 = (l * NF + u0) * 128, (l * NF + u1) * 128
                dst, src = sU[r0:r1, 0:2048], wU[r0:r1, :]
            else:
                r0, r1 = (l * 8 + u0) * 128, (l * 8 + u1) * 128
                dst = sD[r0:r1, :].rearrange("r (two h) -> (r two) h", two=2)
                src = wD[r0:r1, :].rearrange("r (two h) -> (r two) h", two=2)
            POOL.h.dma_start(out=dst, in_=src).then_inc(sem.h, 16)
            for u in range(u0, u1):
                cast_tokens[(name, l, u)] = (sem, 16)

        def pieces_of(name, l, n):
            return [(name, l, a, min(a + 2, n)) for a in range(0, n, 2)]

        cast_sched = {}
        pieces0 = pieces_of("U", 0, NF) + pieces_of("D", 0, 8) + pieces_of("A", 1, 10) + pieces_of("O", 1, 4)
        pieces1 = pieces_of("U", 1, NF) + pieces_of("D", 1, 8)
        for si_, pcs in ((0, pieces0), (2, pieces1)):
            per = -(-len(pcs) // (NT * 3))
            k = 0
            for tl in range(NT):
                for pt in range(3):
                    cast_sched[(si_, tl, pt)] = pcs[k:k + per]
                    k += per
            assert k >= len(pcs)

        for a in range(0, 10, 2):
            cast_piece("A", 0, a, a + 2)

        def prologue_casts_rest():
            POOL.wait((s_x[0], 16))
            dma(POOL, s_x[1], xt(1), xT_v[:, :, T:2 * T], writes=[xb[1]])
            cast_piece("O", 0, 0, 2)
            cast_piece("O", 0, 2, 4)
            dma(POOL, s_x[2], xt(2), xT_v[:, :, 2 * T:3 * T], writes=[xb[2]])

        XO = NX0 * NCH * T
        R1 = xs[:, XO:XO + 8192].bitcast(BF16)
        RBr = xs[:, XO + 8192:XO + 8192 + 768].bitcast(BF16)
        RFr = xs[:, XO + 8192 + 768:XO + 8192 + 768 + 8448].bitcast(BF16)
        r1b, rbb, rfb = Buf(), Buf(), Buf()
        r1a = Buf()
        RA = R1.rearrange("p (j k n) -> p j k n", j=4, k=32)
        RB = RBr.rearrange("p (k n) -> p k n", k=12)
        RF = RFr.rearrange("p (k n) -> p k n", k=132)
        s_dg = {}

        def dg_tok(name, l):
            sem = s_dg[(name, l)]
            return (sem, sem.total)

        def bcast_build(eng, out, c0, n):
            return eng.h.tensor_tensor(out=out, in0=ident[:].unsqueeze(1).to_broadcast([128, n, 128]),
                                       in1=prm[:, c0:c0 + n].unsqueeze(2).to_broadcast([128, n, 128]), op=ALU.mult)

        def build_A(eng, l, reg, regb, js=(0, 1, 2, 3), store=True, regb2=None):
            ra = reg.rearrange("p (j k n) -> p j k n", j=4, k=32)
            acquire(eng, [constb, prmb], [regb])
            ins = None
            for j in js:
                bcast_build(eng, ra[:, j, 0:31, :], l * PL + P_CAW + j * 31, 31)
                ins = eng.h.memset(ra[:, j, 31, :], 0.0)
            release(eng.sig(ins), eng.name, [constb, prmb], [regb])
            if not store:
                return
            if regb2 is not None:
                POOL.wait(regb2.w)
            sem = mksem(f"s_dgA{l}")
            s_dg[("A", l)] = sem
            dma(POOL, sem, sCA[l * 1024:(l + 1) * 1024, :].rearrange("(u p) n -> p u n", p=128),
                reg.rearrange("p (u n) -> p u n", u=8), reads=[regb])

        def build_B(eng, l):
            acquire(eng, [constb, prmb], [rbb])
            ins = bcast_build(eng, RB, l * PL + P_CBW, 12)
            release(eng.sig(ins), eng.name, [constb, prmb], [rbb])
            sem = mksem(f"s_dgB{l}")
            s_dg[("B", l)] = sem
            dma(POOL, sem, sCB[l * 128:(l + 1) * 128, :], RBr, reads=[rbb])

        def build_F(eng, l):
            acquire(eng, [constb, prmb], [rfb])
            ins = bcast_build(eng, RF, l * PL + P_CFW, 132)
            release(eng.sig(ins), eng.name, [constb, prmb], [rfb])
            sem = mksem(f"s_dgF{l}")
            s_dg[("F", l)] = sem
            r0, r1 = l * NF * 128, (l + 1) * NF * 128
            dma(POOL, sem, sU[r0:r1, 2048:2816].rearrange("(j p) n -> p j n", p=128),
                RFr.rearrange("p (j n) -> p j n", j=NF), reads=[rfb])

        def prologue_builds():
            build_A(DVE, 0, R1, r1a, js=(0, 1), store=False)
            build_B(DVE, 0)
            build_A(POOL, 0, R1, r1b, js=(2, 3), store=True, regb2=r1a)
            prologue_casts_rest()
            for i in (3, 4):
                dma(POOL, s_x[i], xt(i), xT_v[:, :, i * T:(i + 1) * T], writes=[xb[i], r1b, r1a])
            build_A(POOL, 1, RFr[:, 0:16384], rfb)
            build_B(POOL, 1)
            for i in (5, 6, 7):
                dma(POOL, s_x[i], xt(i), xT_v[:, :, i * T:(i + 1) * T], writes=[xb[i], rbb, rfb])

        pend = {}
        late_unreserve = []

        def flush_unreserve():
            while late_unreserve:
                reserved.discard(late_unreserve.pop())

        def norm_p1_steps(key, i):
            x_i = xt(i)

            def step(k):
                if k < NCH:
                    q = k % 2
                    op(ACT, lambda: nc.scalar.activation(out=sqb_t[:, q, :], in_=x_i[:, k, :], func=AF.Square),
                       reads=[xb[i]], writes=[sqbb[q]])
                if k >= 1:
                    c = k - 1
                    if c == 0:
                        b = alloc_bank()
                        reserved.add(b)
                        pend[key] = b
                    b = pend[key]
                    q = c % 2
                    mm(b, ones1024[:], sqb_t[:, q, :], c == 0, c == NCH - 1, [sqbb[q], constb, prmb], True)
            return [(lambda k=k: step(k)) for k in range(NCH + 1)]

        def norm_p2(key, i, gcol, out_fn, out_bufs_w):
            b = pend.pop(key)
            x_i = xt(i)
            op(ACT, lambda: nc.scalar.activation(out=sd_t[:], in_=ps[:, b, :], func=AF.Sqrt, bias=epsc[:, 0:1], scale=1.0),
               reads=[pb[b], constb, prmb], writes=[sdb])
            op(DVE, lambda: nc.vector.reciprocal(out=ps[:, b, :], in_=sd_t[:]), reads=[sdb], writes=[pb[b]])
            for c in range(NCH):
                op(DVE, lambda: nc.vector.scalar_tensor_tensor(out=out_fn(c), in0=x_i[:, c, :], scalar=pcol(gcol + c),
                                                               in1=ps[:, b, :], op0=ALU.mult, op1=ALU.mult),
                   reads=[xb[i], pb[b], constb, prmb], writes=out_bufs_w(c))
            late_unreserve.append(b)

        def h_out(c):
            return hbuf[:, c, :]

        def issue_casts(si, i, pt):
            pcs = cast_sched.get((si, i, pt), [])
            if pcs:
                POOL.wait(PE.last())
                for pc in pcs:
                    cast_piece(*pc)

        def mixer_tile(l, i, steps, p2s):
            base = l * PL
            steps = list(steps)

            def inproj_unit(u):
                slot, sbf = next_unit(sA[(l * 10 + u) * 128:(l * 10 + u + 1) * 128, :], 2048, [cast_tokens[("A", l, u)]])
                W = slot[:, 0:2048].rearrange("p (c n) -> p c n", c=NCH)
                banks = []
                for s in range(2):
                    b = alloc_bank()
                    for c in range(NCH):
                        mm(b, W[:, c, s * 128:(s + 1) * 128], hbuf[:, c, :], c == 0, c == NCH - 1, [sbf, hbc[c]], False)
                    banks.append(b)
                bA, bB = banks
                c0 = base + P_BA + 2 * u
                if u < 8:
                    j = u % 4
                    k = next_tmp()
                    fn = AF.Sigmoid if u < 4 else AF.Identity
                    op(ACT, lambda: nc.scalar.activation(out=tmpM[k][:], in_=ps[:, bB, :], func=fn, bias=pcol(c0 + 1), scale=1.0),
                       reads=[pb[bB], constb, prmb], writes=[tmpb[k]])
                    if u < 4:
                        dst, dbuf = a_in[:, j, 30:30 + T], ainb[j]
                    else:
                        dst, dbuf = cx[:, j, 2:2 + T], cxb[j]
                    op(DVE, lambda: nc.vector.scalar_tensor_tensor(out=dst, in0=ps[:, bA, :], scalar=pcol(c0), in1=tmpM[k][:],
                                                                   op0=ALU.add, op1=ALU.mult),
                       reads=[pb[bA], tmpb[k], constb, prmb], writes=[dbuf])
                else:
                    for s in range(2):
                        j = (u - 8) * 2 + s
                        bb = banks[s]
                        op(ACT, lambda: nc.scalar.activation(out=gb[:, j, :], in_=ps[:, bb, :], func=AF.Identity, bias=pcol(c0 + s), scale=1.0),
                           reads=[pb[bb], constb, prmb], writes=[gbb[j]])

            for u in range(4):
                inproj_unit(u)
            for j in range(4):
                b = alloc_bank()
                for half in range(2):
                    uu = l * 8 + j * 2 + half
                    slot, sbf = next_unit(sCA[uu * 128:(uu + 1) * 128, :], 2048, [dg_tok("A", l)])
                    ntap = 16 if half == 0 else 15
                    for kk in range(ntap):
                        kt = half * 16 + kk
                        mm(b, slot[:, kk * 128:(kk + 1) * 128], a_in[:, j, kt:kt + T], kt == 0, kt == 30,
                           [sbf, ainb[j]], kk == ntap - 1)
                        if steps and kk in (4, 9, 14):
                            steps.pop(0)()
                cb = base + P_CAB + j
                op(ACT, lambda: nc.scalar.activation(out=ac_bf[:, j, :], in_=ps[:, b, :], func=AF.Identity, bias=pcol(cb), scale=1.0),
                   reads=[pb[b], constb, prmb], writes=[acb[j]])
                op(ACT, lambda: nc.scalar.activation(out=asq[:, j, :], in_=ps[:, b, :], func=AF.Square, bias=pcol(cb), scale=1.0),
                   reads=[pb[b], constb, prmb], writes=[asqb[j]])
            while steps:
                steps.pop(0)()
            if i + 1 < NT:
                op(DVE, lambda: nc.vector.tensor_copy(out=a_in[:, :, 0:30], in_=a_in[:, :, T:T + 30]), reads=[], writes=ainb)
            issue_casts(cur[0][0], cur[0][1], 1)
            bm = alloc_bank()
            reserved.add(bm)
            bq = alloc_bank()
            reserved.add(bq)
            for j in range(4):
                mm(bm, ones512[:], ac_bf[:, j, :], j == 0, j == 3, [acb[j], constb, prmb], False)
                mm(bq, ones512[:], asq[:, j, :], j == 0, j == 3, [asqb[j], constb, prmb], False)

            def ln_head():
                op(ACT, lambda: nc.scalar.activation(out=var_t[:], in_=ps[:, bm, :], func=AF.Square), reads=[pb[bm]], writes=[varb])
                op(DVE, lambda: nc.vector.tensor_tensor(out=var_t[:], in0=ps[:, bq, :], in1=var_t[:], op=ALU.subtract),
                   reads=[pb[bq], varb], writes=[varb])
                op(ACT, lambda: nc.scalar.activation(out=sd_t[:], in_=var_t[:], func=AF.Sqrt, bias=epsc[:, 0:1], scale=1.0),
                   reads=[varb, constb, prmb], writes=[sdb])
                op(DVE, lambda: nc.vector.reciprocal(out=ps[:, bq, :], in_=sd_t[:]), reads=[sdb], writes=[pb[bq]])

            def ln_j(j):
                k = next_tmp()
                op(DVE, lambda: nc.vector.tensor_tensor(out=tmpM[k][:], in0=ac_bf[:, j, :], in1=ps[:, bm, :], op=ALU.subtract),
                   reads=[acb[j], pb[bm]], writes=[tmpb[k]])
                op(DVE, lambda: nc.vector.tensor_tensor(out=tmpM[k][:], in0=tmpM[k][:], in1=ps[:, bq, :], op=ALU.mult),
                   reads=[tmpb[k], pb[bq]], writes=[tmpb[k]])
                op(ACT, lambda: nc.scalar.activation(out=ybuf[:, j, :], in_=tmpM[k][:], func=AF.Silu,
                                                     bias=pcol(base + P_LNB + j), scale=pcol(base + P_LNG + j)),
                   reads=[tmpb[k], constb, prmb], writes=[yb[j]])

            for u in range(4, 10):
                inproj_unit(u)
                if u == 4:
                    ln_head()
                elif u <= 8:
                    ln_j(u - 5)
            reserved.discard(bm)
            reserved.discard(bq)
            issue_casts(cur[0][0], cur[0][1], 2)
            slot, sbf = next_unit(sCB[l * 128:(l + 1) * 128, :], 1536, [dg_tok("B", l)])
            for j in range(4):
                b = alloc_bank()
                for kt in range(3):
                    mm(b, slot[:, (j * 3 + kt) * 128:(j * 3 + kt + 1) * 128], cx[:, j, kt:kt + T], kt == 0, kt == 2,
                       [sbf, cxb[j]], False)
                op(DVE, lambda: nc.vector.tensor_tensor(out=ybuf[:, 4 + j, :], in0=ps[:, b, :], in1=gb[:, j, :], op=ALU.mult),
                   reads=[pb[b], gbb[j]], writes=[yb[4 + j]])
            if i + 1 < NT:
                op(ACT, lambda: nc.scalar.activation(out=cx[:, :, 0:2], in_=cx[:, :, T:T + 2], func=AF.Identity), reads=[], writes=cxb)
            for f in p2s:
                f()
            x_i = xt(i)
            for u in range(4):
                slot, sbf = next_unit(sO[(l * 4 + u) * 128:(l * 4 + u + 1) * 128, :], 2048, [cast_tokens[("O", l, u)]])
                W = slot[:, 0:2048].rearrange("p (c n) -> p c n", c=NCH)
                for s in range(2):
                    oc = 2 * u + s
                    b = alloc_bank()
                    for c in range(NCH):
                        mm(b, W[:, c, s * 128:(s + 1) * 128], ybuf[:, c, :], c == 0, c == NCH - 1, [sbf, yb[c]], False)
                    op(DVE, lambda: nc.vector.tensor_tensor(out=x_i[:, oc, :], in0=ps[:, b, :], in1=x_i[:, oc, :], op=ALU.add),
                       reads=[pb[b]], writes=[xb[i]])

        def ffn_tile(l, i, steps, p2s, tail):
            units = {}
            steps = list(steps)

            cf = prm[:, l * PL + P_CFW:l * PL + P_CFW + 6 * NF].rearrange("p (q k) -> p q k", k=3)
            op(DVE, lambda: nc.vector.tensor_tensor(out=contrib[:, :, 1], in0=halu[:, :, 1], in1=cf[:, :, 0], op=ALU.mult),
               reads=halub + [prmb], writes=[contribb])
            op(DVE, lambda: nc.vector.tensor_tensor(out=contrib[:, :, 0], in0=halu[:, :, 0], in1=cf[:, :, 0], op=ALU.mult),
               reads=halub + [prmb], writes=[contribb])
            op(DVE, lambda: nc.vector.tensor_tensor(out=tmp44, in0=halu[:, :, 1], in1=cf[:, :, 1], op=ALU.mult),
               reads=halub + [prmb], writes=[contribb])
            op(DVE, lambda: nc.vector.tensor_tensor(out=contrib[:, :, 0], in0=contrib[:, :, 0], in1=tmp44, op=ALU.add),
               reads=[contribb], writes=[contribb])

            def conv_taps(b, acc, accbuf, q, c0):
                op(ACT, lambda: nc.scalar.activation(out=acc[:, 0:T], in_=ps[:, b, :], func=AF.Identity, scale=pcol(c0 + 2)),
                   reads=[pb[b], prmb], writes=[accbuf])
                op(ACT, lambda: nc.scalar.activation(out=halu[:, q, :], in_=ps[:, b, T - 2:T], func=AF.Identity),
                   reads=[pb[b]], writes=[halub[q]])
                op(DVE, lambda: nc.vector.scalar_tensor_tensor(out=acc[:, 1:T], in0=ps[:, b, 0:T - 1], scalar=pcol(c0 + 1),
                                                               in1=acc[:, 1:T], op0=ALU.mult, op1=ALU.add),
                   reads=[pb[b], accbuf, prmb], writes=[accbuf])
                op(DVE, lambda: nc.vector.scalar_tensor_tensor(out=acc[:, 2:T], in0=ps[:, b, 0:T - 2], scalar=pcol(c0),
                                                               in1=acc[:, 2:T], op0=ALU.mult, op1=ALU.add),
                   reads=[pb[b], accbuf, prmb], writes=[accbuf])
                op(POOL, lambda: nc.gpsimd.tensor_tensor(out=acc[:, 0:2], in0=acc[:, 0:2], in1=contrib[:, q, :], op=ALU.add),
                   reads=[contribb, accbuf], writes=[accbuf])

            def up(j):
                slot, sbf = next_unit(sU[(l * NF + j) * 128:(l * NF + j + 1) * 128, 0:2048], 2048, [cast_tokens[("U", l, j)]])
                W = slot[:, 0:2048].rearrange("p (c n) -> p c n", c=NCH)
                bks = []
                for s_ in range(2):
                    b = alloc_bank()
                    for c in range(NCH):
                        mm(b, W[:, c, s_ * 128:(s_ + 1) * 128], hbuf[:, c, :], c == 0, c == NCH - 1, [sbf, hbc[c]], False)
                    bks.append(b)
                bG, bV = bks
                kg, kv = j % 3, j % 2
                c0 = l * PL + P_CFW + j * 6
                conv_taps(bG, accg[kg], accgb[kg], 2 * j, c0)
                conv_taps(bV, accv[kv], accvb[kv], 2 * j + 1, c0 + 3)
                units[j] = (kg, kv)

            def conv(j):
                kg, kv = units.pop(j)
                op(ACT, lambda: nc.scalar.activation(out=accg[kg][:], in_=accg[kg][:], func=AF.Silu),
                   reads=[accgb[kg]], writes=[accgb[kg]])
                op(POOL, lambda: nc.gpsimd.tensor_tensor(out=gbuf[:, j, :], in0=accv[kv][:], in1=accg[kg][:], op=ALU.mult),
                   reads=[accvb[kv], accgb[kg]], writes=[gfb[j]])

            for j in range(NF + 1):
                if j < NF:
                    up(j)
                if j >= 1:
                    conv(j - 1)
                if steps and j >= 2:
                    steps.pop(0)()
            while steps:
                steps.pop(0)()
            for f in p2s:
                f()
            x_i = xt(i)
            for oc in range(NCH):
                slot, sbf = next_unit(sD[(l * 8 + oc) * 128:(l * 8 + oc + 1) * 128, :], 2816, [cast_tokens[("D", l, oc)]])
                b = alloc_bank()
                for fc in range(NF):
                    mm(b, slot[:, fc * 128:(fc + 1) * 128], gbuf[:, fc, :], fc == 0, fc == NF - 1, [sbf, gfb[fc]], False)
                op(DVE, lambda: nc.vector.tensor_tensor(out=x_i[:, oc, :], in0=ps[:, b, :], in1=x_i[:, oc, :], op=ALU.add),
                   reads=[pb[b]], writes=[xb[i]])
            for f in tail:
                f()

        def barrier():
            toks = [e.last() for e in engines]
            for e in engines:
                for t in toks:
                    if t is not None and t[0] is not e.sem:
                        e.wait(t)

        stages = []
        for l in range(L):
            stages.append(("M", l, l * PL + P_G1))
            stages.append(("F", l, l * PL + P_G2))

        s_out = mksem("s_out")

        def final_p2(i):
            x_i = xt(i)
            norm_p2(("fin", i), i, P_GF, lambda c: x_i[:, c, :], lambda c: [xb[i]])

        def final_store(i):
            dma(SP, s_out, outT_v[:, :, i * T:(i + 1) * T], xt(i), reads=[xb[i]])

        prologue_builds()
        for f in norm_p1_steps((0, 0), 0):
            f()
        norm_p2((0, 0), 0, stages[0][2], h_out, lambda c: [hbc[c]])
        flush_unreserve()
        last_si = len(stages) - 1
        cur = [None]
        for si, (kind, l, gcol) in enumerate(stages):
            if kind == "M":
                op(DVE, lambda: nc.vector.memset(a_in[:, :, 0:30], 0.0), writes=ainb)
                op(DVE, lambda: nc.vector.memset(cx[:, :, 0:2], 0.0), writes=cxb)
            else:
                op(DVE, lambda: nc.vector.memset(halu, 0.0), writes=halub)
            for i in range(NT):
                cur[0] = (si, i)
                issue_casts(si, i, 0)
                steps, p2s, tail = [], [], []
                if i + 1 < NT:
                    nk, ni, ng = (si, i + 1), i + 1, gcol
                elif si + 1 < len(stages):
                    nk, ni, ng = (si + 1, 0), 0, stages[si + 1][2]
                else:
                    nk = None
                if nk is not None:
                    steps += norm_p1_steps(nk, ni)
                    p2s.append(lambda nk=nk, ni=ni, ng=ng: norm_p2(nk, ni, ng, h_out, lambda c: [hbc[c]]))
                if si == last_si and i >= 1:
                    steps += norm_p1_steps(("fin", i - 1), i - 1)
                    p2s.append(lambda i=i: final_p2(i - 1))
                    tail.append(lambda i=i: final_store(i - 1))
                if kind == "M":
                    mixer_tile(l, i, steps, p2s)
                else:
                    ffn_tile(l, i, steps, p2s, tail)
                flush_unreserve()
            barrier()
        for f in norm_p1_steps(("fin", NT - 1), NT - 1):
            f()
        final_p2(NT - 1)
        final_store(NT - 1)
        SP.wait((s_out, s_out.total))
    return nc


_CACHE = {}


def kernel(**inputs):
    x = np.asarray(inputs["x"], np.float32)
    B = x.shape[0]
    assert x.shape == (8, S, D)
    prm = _prep_params({k: np.asarray(v, np.float32) for k, v in inputs.items() if k not in ("x",)})
    wA, wO, wU, wD = _prep_weights(inputs)
    ident = np.eye(128, dtype=np.float32)
    if "nc" not in _CACHE:
        _CACHE["nc"] = build_program()
    nc = _CACHE["nc"]
    in_maps = []
    for b in range(B):
        in_maps.append({"xT": np.ascontiguousarray(x[b].T), "wA": wA, "wO": wO, "wU": wU, "wD": wD,
                        "prm": prm, "ident": ident})
    res = run_bass_kernel_spmd(nc, in_maps, core_ids=list(range(B)))
    out = np.stack([np.ascontiguousarray(res.results[b]["outT"].T) for b in range(B)], axis=0)
    return out.astype(np.float32)
```

```python
import numpy as np
import concourse.bass as bass
import concourse.mybir as mybir
from concourse.bass_utils import run_bass_kernel_spmd

F32 = mybir.dt.float32
BF16 = mybir.dt.bfloat16
AF = mybir.ActivationFunctionType
ALU = mybir.AluOpType

D = 1024
S = 4096
L = 2
DFF = 2816
T = 512
NT = S // T
NCH = 8
NF = 22
EPS = 1e-6
NSLOT = 5
SLOT = 2816

IN_ORDER = []
for _j in range(4):
    IN_ORDER += [_j, 4 + _j]
for _j in range(4):
    IN_ORDER += [12 + _j, 16 + _j]
IN_ORDER += [8, 9, 10, 11]

P_G1 = 0
P_BA = 8
P_CAW = 28
P_CAB = 152
P_LNG = 156
P_LNB = 160
P_CBW = 164
P_G2 = 176
P_CFW = 184
PL = 316
P_GF = 2 * PL
NPRM = 640


def _cols(v):
    v = np.asarray(v, dtype=np.float32).reshape(-1, 128)
    return v.T


def _prep_params(inp):
    cols = []
    for l in range(L):
        cols.append(_cols(inp["mix_norm_g"][l]))
        cols.append(_cols(inp["b_in"][l].reshape(20, 128)[IN_ORDER]))
        caw = inp["conv_a_w"][l].reshape(31, 4, 128).transpose(1, 0, 2)
        cols.append(_cols(caw))
        cols.append(_cols(inp["conv_a_b"][l]))
        cols.append(_cols(inp["ln_a_g"][l]))
        cols.append(_cols(inp["ln_a_b"][l]))
        cbw = inp["conv_b_w"][l].reshape(3, 4, 128).transpose(1, 0, 2)
        cols.append(_cols(cbw))
        cols.append(_cols(inp["ffn_norm_g"][l]))
        cfw = inp["conv_f_w"][l].reshape(3, 2, NF, 128).transpose(2, 1, 0, 3)
        cols.append(_cols(cfw))
    cols.append(_cols(inp["final_norm_g"]))
    prm = np.ascontiguousarray(np.concatenate(cols, axis=1), dtype=np.float32)
    assert prm.shape == (128, NPRM), prm.shape
    return prm


def _prep_weights(inp):
    w_in = np.asarray(inp["w_in"], np.float32)
    w = w_in.reshape(L, 8, 128, 20, 128)[:, :, :, IN_ORDER, :]
    w = w.reshape(L, 8, 128, 10, 2, 128).transpose(0, 3, 2, 1, 4, 5)
    wA = np.ascontiguousarray(w).reshape(L * 1280, 2048)
    w = np.asarray(inp["w_out"], np.float32).reshape(L, 8, 128, 4, 2, 128).transpose(0, 3, 2, 1, 4, 5)
    wO = np.ascontiguousarray(w).reshape(L * 512, 2048)
    w = np.asarray(inp["w_up"], np.float32).reshape(L, 8, 128, 2, NF, 128).transpose(0, 4, 2, 1, 3, 5)
    wU = np.ascontiguousarray(w).reshape(L * NF * 128, 2048)
    w = np.asarray(inp["w_down"], np.float32).reshape(L, NF, 128, 8, 128).transpose(0, 3, 2, 1, 4)
    wD = np.ascontiguousarray(w).reshape(L * 1024, 2816)
    return wA, wO, wU, wD


class Sem:
    def __init__(self, h, name):
        self.h = h
        self.name = name
        self.total = 0


class Eng:
    def __init__(self, h, sem, name):
        self.h = h
        self.sem = sem
        self.name = name
        self.n = 0
        self.seen = {}

    def wait(self, tok):
        if tok is None:
            return
        sem, val = tok
        if self.seen.get(sem.name, 0) >= val:
            return
        self.h.wait_ge(sem.h, val)
        self.seen[sem.name] = val

    def sig(self, ins):
        self.n += 1
        ins.then_inc(self.sem.h, 1)
        return (self.sem, self.n)

    def last(self):
        return (self.sem, self.n) if self.n else None


class Buf:
    def __init__(self):
        self.w = None
        self.r = {}


def acquire(eng, reads=(), writes=()):
    for b in reads:
        eng.wait(b.w)
    for b in writes:
        eng.wait(b.w)
        for t in b.r.values():
            eng.wait(t)


def release(tok, who, reads=(), writes=()):
    for b in reads:
        b.r[who] = tok
    for b in writes:
        b.w = tok
        b.r = {}


def build_program():
    nc = bass.Bass("TRN2", target_bir_lowering=False)
    dt = nc.dram_tensor
    xT = dt("xT", [D, S], F32, kind="ExternalInput").ap()
    wA = dt("wA", [L * 1280, 2048], F32, kind="ExternalInput").ap()
    wO = dt("wO", [L * 512, 2048], F32, kind="ExternalInput").ap()
    wU = dt("wU", [L * NF * 128, 2048], F32, kind="ExternalInput").ap()
    wD = dt("wD", [L * 1024, 2816], F32, kind="ExternalInput").ap()
    prm_d = dt("prm", [128, NPRM], F32, kind="ExternalInput").ap()
    ident_d = dt("ident", [128, 128], F32, kind="ExternalInput").ap()
    outT = dt("outT", [D, S], F32, kind="ExternalOutput").ap()
    sA = dt("sA", [L * 1280, 2048], BF16, kind="Internal").ap()
    sO = dt("sO", [L * 512, 2048], BF16, kind="Internal").ap()
    sU = dt("sU", [L * NF * 128, 2816], BF16, kind="Internal").ap()
    sD = dt("sD", [L * 1024, 2816], BF16, kind="Internal").ap()
    sCA = dt("sCA", [L * 1024, 2048], BF16, kind="Internal").ap()
    sCB = dt("sCB", [L * 128, 1536], BF16, kind="Internal").ap()

    xT_v = xT.rearrange("(c p) t -> p c t", p=128)
    outT_v = outT.rearrange("(c p) t -> p c t", p=128)

    import contextlib
    es = contextlib.ExitStack()
    with es:
        def sb(name, shape, dtype):
            return es.enter_context(nc.sbuf_tensor(name, shape, dtype))

        def mksem(name):
            return Sem(es.enter_context(nc.semaphore(name)), name)

        xs = sb("xs", [128, NT * NCH * T], F32)
        ring = sb("ring", [128, NSLOT * SLOT], BF16)
        hbuf = sb("hbuf", [128, NCH, T], BF16)
        sqb_t = sb("sqb", [128, 2, T], BF16)
        prm = sb("prm_sb", [128, NPRM], F32)
        ident = sb("ident_sb", [128, 128], F32)
        ones1024 = sb("ones1024", [128, 128], BF16)
        ones512 = sb("ones512", [128, 128], BF16)
        epsc = sb("epsc", [128, 1], F32)
        sd_t = sb("sd", [128, T], F32)
        var_t = sb("var", [128, T], F32)
        stg = sb("stg", [128, 17408], BF16)
        ps = es.enter_context(nc.psum_tensor("ps", [128, 8, T], F32))

        def xt(i):
            return xs[:, i * NCH * T:(i + 1) * NCH * T].rearrange("p (c t) -> p c t", c=NCH)

        o = 0
        a_in = stg[:, o:o + 4 * 542].rearrange("p (j t) -> p j t", j=4); o += 4 * 542
        cx = stg[:, o:o + 4 * 514].rearrange("p (j t) -> p j t", j=4); o += 4 * 514
        gb = stg[:, o:o + 4 * T].rearrange("p (j t) -> p j t", j=4); o += 4 * T
        ac_bf = stg[:, o:o + 4 * T].rearrange("p (j t) -> p j t", j=4); o += 4 * T
        asq = stg[:, o:o + 4 * T].rearrange("p (j t) -> p j t", j=4); o += 4 * T
        ybuf = stg[:, o:o + 8 * T].rearrange("p (j t) -> p j t", j=8); o += 8 * T
        tmpM = [stg[:, o + k * 2 * T:o + (k + 1) * 2 * T].bitcast(F32) for k in range(2)]; o += 4 * T
        assert o <= 17408, o
        o = 0
        gbuf = stg[:, o:o + NF * T].rearrange("p (j t) -> p j t", j=NF); o += NF * T
        halu = stg[:, o:o + NF * 8].bitcast(F32).rearrange("p (q t) -> p q t", t=2); o += NF * 8
        contrib = stg[:, o:o + NF * 8].bitcast(F32).rearrange("p (q t) -> p q t", t=2); o += NF * 8
        tmp44 = stg[:, o:o + NF * 4].bitcast(F32); o += NF * 4
        accg = [stg[:, o + k * 2 * T:o + (k + 1) * 2 * T].bitcast(F32) for k in range(3)]; o += 6 * T
        accv = [stg[:, o + k * 2 * T:o + (k + 1) * 2 * T].bitcast(F32) for k in range(2)]; o += 4 * T
        assert o <= 17408, o

        PE = Eng(nc.tensor, mksem("s_pe"), "PE")
        ACT = Eng(nc.scalar, mksem("s_act"), "ACT")
        DVE = Eng(nc.vector, mksem("s_dve"), "DVE")
        POOL = Eng(nc.gpsimd, mksem("s_pool"), "POOL")
        SP = Eng(nc.sync, mksem("s_sp"), "SP")
        engines = [PE, ACT, DVE, POOL, SP]

        def op(eng, fn, reads=(), writes=()):
            acquire(eng, reads, writes)
            tok = eng.sig(fn())
            release(tok, eng.name, reads, writes)
            return tok

        def dma(q, sem, out, in_, reads=(), writes=(), extra=()):
            acquire(q, reads, writes)
            for t in extra:
                q.wait(t)
            q.h.dma_start(out=out, in_=in_).then_inc(sem.h, 16)
            sem.total += 16
            tok = (sem, sem.total)
            release(tok, "dma:" + sem.name, reads, writes)
            return tok

        xb = [Buf() for _ in range(NT)]
        slotb = [Buf() for _ in range(NSLOT)]
        hb = Buf()
        sqbb = [Buf(), Buf()]
        pb = [Buf() for _ in range(8)]
        sdb, rstdb, meanb, varb = Buf(), Buf(), Buf(), Buf()
        ainb = [Buf() for _ in range(4)]
        cxb = [Buf() for _ in range(4)]
        gbb = [Buf() for _ in range(4)]
        acb = [Buf() for _ in range(4)]
        asqb = [Buf() for _ in range(4)]
        yb = [Buf() for _ in range(8)]
        tmpb = [Buf(), Buf()]
        gfb = [Buf() for _ in range(NF)]
        ubm = [Buf() for _ in range(4)]
        ubh = [Buf() for _ in range(4)]
        halb = [Buf() for _ in range(NF)]
        halvb = [Buf() for _ in range(NF)]
        accgb = [Buf(), Buf(), Buf()]
        accvb = [Buf(), Buf()]
        halub = [Buf() for _ in range(2 * NF)]
        contribb = Buf()
        constb = Buf()
        prmb = Buf()
        hbc = [Buf() for _ in range(NCH)]
        pe_pend = []
        state = {"bank": 0, "unit": 0, "tmp": 0}

        def next_tmp():
            k = state["tmp"]
            state["tmp"] = (k + 1) % 2
            return k

        s_slot = [mksem(f"s_slot{k}") for k in range(NSLOT)]

        def next_unit(src, n, deps):
            k = state["unit"] % NSLOT
            state["unit"] += 1
            dst = ring[:, k * SLOT:k * SLOT + n]
            dma(SP, s_slot[k], dst, src, writes=[slotb[k]], extra=deps)
            return ring[:, k * SLOT:(k + 1) * SLOT], slotb[k]

        def mm(b, lhsT, rhs, start, stop, reads, signal):
            acquire(PE, reads, [pb[b]] if start else [])
            ins = nc.tensor.matmul(ps[:, b, :], lhsT=lhsT, rhs=rhs, start=start, stop=stop)
            pe_pend.extend(reads)
            if signal or stop:
                tok = PE.sig(ins)
                release(tok, "PE", list(pe_pend), [pb[b]] if stop else [])
                del pe_pend[:]

        def pcol(c):
            return prm[:, c:c + 1]

        reserved = set()

        def alloc_bank():
            while True:
                b = state["bank"]
                state["bank"] = (b + 1) % 8
                if b not in reserved:
                    return b

        s_c = mksem("s_const")
        s_x = [mksem(f"s_x{i}") for i in range(NT)]
        t_prm = dma(SP, s_c, prm[:], prm_d[:], writes=[prmb])
        t_prm = dma(SP, s_c, ident[:], ident_d[:], writes=[prmb])
        NX0 = 3
        for i in range(NX0):
            dma(ACT, s_x[i], xt(i), xT_v[:, :, i * T:(i + 1) * T], writes=[xb[i]])
        op(DVE, lambda: nc.vector.memset(ones1024[:], 1.0 / 1024.0), writes=[constb])
        op(DVE, lambda: nc.vector.memset(ones512[:], 1.0 / 512.0), writes=[constb])
        op(DVE, lambda: nc.vector.memset(epsc[:], EPS), writes=[constb])

        cast_tokens = {}

        def cast_piece(name, l, u0, u1):
            sem = mksem(f"s_cast_{name}{l}_{u0}")
            if name == "A":
                r0, r1 = (l * 10 + u0) * 128, (l * 10 + u1) * 128
                dst, src = sA[r0:r1, :], wA[r0:r1, :]
            elif name == "O":
                r0, r1 = (l * 4 + u0) * 128, (l * 4 + u1) * 128
                dst, src = sO[r0:r1, :], wO[r0:r1, :]
            elif name == "U":
                r0, r1 = (l * NF + u0) * 128, (l * NF + u1) * 128
                dst, src = sU[r0:r1, 0:2048], wU[r0:r1, :]
            else:
                r0, r1 = (l * 8 + u0) * 128, (l * 8 + u1) * 128
                dst = sD[r0:r1, :].rearrange("r (two h) -> (r two) h", two=2)
                src = wD[r0:r1, :].rearrange("r (two h) -> (r two) h", two=2)
            POOL.h.dma_start(out=dst, in_=src).then_inc(sem.h, 16)
            for u in range(u0, u1):
                cast_tokens[(name, l, u)] = (sem, 16)

        def pieces_of(name, l, n):
            return [(name, l, a, min(a + 2, n)) for a in range(0, n, 2)]

        cast_sched = {}
        pieces0 = pieces_of("U", 0, NF) + pieces_of("D", 0, 8) + pieces_of("A", 1, 10) + pieces_of("O", 1, 4)
        pieces1 = pieces_of("U", 1, NF) + pieces_of("D", 1, 8)
        for si_, pcs in ((0, pieces0), (2, pieces1)):
            per = -(-len(pcs) // (NT * 3))
            k = 0
            for tl in range(NT):
                for pt in range(3):
                    cast_sched[(si_, tl, pt)] = pcs[k:k + per]
                    k += per
            assert k >= len(pcs)

        for a in range(0, 10, 2):
            cast_piece("A", 0, a, a + 2)
        cast_piece("O", 0, 0, 2)
        cast_piece("O", 0, 2, 4)

        XO = NX0 * NCH * T
        R1 = xs[:, XO:XO + 8192].bitcast(BF16)
        RBr = xs[:, XO + 8192:XO + 8192 + 768].bitcast(BF16)
        RFr = xs[:, XO + 8192 + 768:XO + 8192 + 768 + 8448].bitcast(BF16)
        r1b, rbb, rfb = Buf(), Buf(), Buf()
        r1a = Buf()
        RA = R1.rearrange("p (j k n) -> p j k n", j=4, k=32)
        RB = RBr.rearrange("p (k n) -> p k n", k=12)
        RF = RFr.rearrange("p (k n) -> p k n", k=132)
        s_dg = {}

        def dg_tok(name, l):
            sem = s_dg[(name, l)]
            return (sem, sem.total)

        def bcast_build(eng, out, c0, n):
            return eng.h.tensor_tensor(out=out, in0=ident[:].unsqueeze(1).to_broadcast([128, n, 128]),
                                       in1=prm[:, c0:c0 + n].unsqueeze(2).to_broadcast([128, n, 128]), op=ALU.mult)

        def build_A(eng, l, reg, regb, js=(0, 1, 2, 3), store=True, regb2=None):
            ra = reg.rearrange("p (j k n) -> p j k n", j=4, k=32)
            acquire(eng, [constb, prmb], [regb])
            ins = None
            for j in js:
                bcast_build(eng, ra[:, j, 0:31, :], l * PL + P_CAW + j * 31, 31)
                ins = eng.h.memset(ra[:, j, 31, :], 0.0)
            release(eng.sig(ins), eng.name, [constb, prmb], [regb])
            if not store:
                return
            if regb2 is not None:
                POOL.wait(regb2.w)
            sem = mksem(f"s_dgA{l}")
            s_dg[("A", l)] = sem
            dma(POOL, sem, sCA[l * 1024:(l + 1) * 1024, :].rearrange("(u p) n -> p u n", p=128),
                reg.rearrange("p (u n) -> p u n", u=8), reads=[regb])

        def build_B(eng, l):
            acquire(eng, [constb, prmb], [rbb])
            ins = bcast_build(eng, RB, l * PL + P_CBW, 12)
            release(eng.sig(ins), eng.name, [constb, prmb], [rbb])
            sem = mksem(f"s_dgB{l}")
            s_dg[("B", l)] = sem
            dma(POOL, sem, sCB[l * 128:(l + 1) * 128, :], RBr, reads=[rbb])

        def build_F(eng, l):
            acquire(eng, [constb, prmb], [rfb])
            ins = bcast_build(eng, RF, l * PL + P_CFW, 132)
            release(eng.sig(ins), eng.name, [constb, prmb], [rfb])
            sem = mksem(f"s_dgF{l}")
            s_dg[("F", l)] = sem
            r0, r1 = l * NF * 128, (l + 1) * NF * 128
            dma(POOL, sem, sU[r0:r1, 2048:2816].rearrange("(j p) n -> p j n", p=128),
                RFr.rearrange("p (j n) -> p j n", j=NF), reads=[rfb])

        def prologue_builds():
            build_A(DVE, 0, R1, r1a, js=(0, 1), store=False)
            build_B(DVE, 0)
            build_A(POOL, 0, R1, r1b, js=(2, 3), store=True, regb2=r1a)
            for i in (3, 4):
                dma(POOL, s_x[i], xt(i), xT_v[:, :, i * T:(i + 1) * T], writes=[xb[i], r1b, r1a])
            build_A(POOL, 1, RFr[:, 0:16384], rfb)
            build_B(POOL, 1)
            for i in (5, 6, 7):
                dma(POOL, s_x[i], xt(i), xT_v[:, :, i * T:(i + 1) * T], writes=[xb[i], rbb, rfb])

        pend = {}
        late_unreserve = []

        def flush_unreserve():
            while late_unreserve:
                reserved.discard(late_unreserve.pop())

        def norm_p1_steps(key, i):
            x_i = xt(i)

            def step(k):
                if k < NCH:
                    q = k % 2
                    op(ACT, lambda: nc.scalar.activation(out=sqb_t[:, q, :], in_=x_i[:, k, :], func=AF.Square),
                       reads=[xb[i]], writes=[sqbb[q]])
                if k >= 1:
                    c = k - 1
                    if c == 0:
                        b = alloc_bank()
                        reserved.add(b)
                        pend[key] = b
                    b = pend[key]
                    q = c % 2
                    mm(b, ones1024[:], sqb_t[:, q, :], c == 0, c == NCH - 1, [sqbb[q], constb, prmb], True)
            return [(lambda k=k: step(k)) for k in range(NCH + 1)]

        def norm_p2(key, i, gcol, out_fn, out_bufs_w):
            b = pend.pop(key)
            x_i = xt(i)
            op(ACT, lambda: nc.scalar.activation(out=sd_t[:], in_=ps[:, b, :], func=AF.Sqrt, bias=epsc[:, 0:1], scale=1.0),
               reads=[pb[b], constb, prmb], writes=[sdb])
            op(DVE, lambda: nc.vector.reciprocal(out=ps[:, b, :], in_=sd_t[:]), reads=[sdb], writes=[pb[b]])
            for c in range(NCH):
                op(DVE, lambda: nc.vector.scalar_tensor_tensor(out=out_fn(c), in0=x_i[:, c, :], scalar=pcol(gcol + c),
                                                               in1=ps[:, b, :], op0=ALU.mult, op1=ALU.mult),
                   reads=[xb[i], pb[b], constb, prmb], writes=out_bufs_w(c))
            late_unreserve.append(b)

        def h_out(c):
            return hbuf[:, c, :]

        def issue_casts(si, i, pt):
            pcs = cast_sched.get((si, i, pt), [])
            if pcs:
                POOL.wait(PE.last())
                for pc in pcs:
                    cast_piece(*pc)

        def mixer_tile(l, i, steps, p2s):
            base = l * PL
            steps = list(steps)

            def inproj_unit(u):
                slot, sbf = next_unit(sA[(l * 10 + u) * 128:(l * 10 + u + 1) * 128, :], 2048, [cast_tokens[("A", l, u)]])
                W = slot[:, 0:2048].rearrange("p (c n) -> p c n", c=NCH)
                banks = []
                for s in range(2):
                    b = alloc_bank()
                    for c in range(NCH):
                        mm(b, W[:, c, s * 128:(s + 1) * 128], hbuf[:, c, :], c == 0, c == NCH - 1, [sbf, hbc[c]], False)
                    banks.append(b)
                bA, bB = banks
                c0 = base + P_BA + 2 * u
                if u < 8:
                    j = u % 4
                    k = next_tmp()
                    fn = AF.Sigmoid if u < 4 else AF.Identity
                    op(ACT, lambda: nc.scalar.activation(out=tmpM[k][:], in_=ps[:, bB, :], func=fn, bias=pcol(c0 + 1), scale=1.0),
                       reads=[pb[bB], constb, prmb], writes=[tmpb[k]])
                    if u < 4:
                        dst, dbuf = a_in[:, j, 30:30 + T], ainb[j]
                    else:
                        dst, dbuf = cx[:, j, 2:2 + T], cxb[j]
                    op(DVE, lambda: nc.vector.scalar_tensor_tensor(out=dst, in0=ps[:, bA, :], scalar=pcol(c0), in1=tmpM[k][:],
                                                                   op0=ALU.add, op1=ALU.mult),
                       reads=[pb[bA], tmpb[k], constb, prmb], writes=[dbuf])
                else:
                    for s in range(2):
                        j = (u - 8) * 2 + s
                        bb = banks[s]
                        op(ACT, lambda: nc.scalar.activation(out=gb[:, j, :], in_=ps[:, bb, :], func=AF.Identity, bias=pcol(c0 + s), scale=1.0),
                           reads=[pb[bb], constb, prmb], writes=[gbb[j]])

            for u in range(4):
                inproj_unit(u)
            for j in range(4):
                b = alloc_bank()
                for half in range(2):
                    uu = l * 8 + j * 2 + half
                    slot, sbf = next_unit(sCA[uu * 128:(uu + 1) * 128, :], 2048, [dg_tok("A", l)])
                    ntap = 16 if half == 0 else 15
                    for kk in range(ntap):
                        kt = half * 16 + kk
                        mm(b, slot[:, kk * 128:(kk + 1) * 128], a_in[:, j, kt:kt + T], kt == 0, kt == 30,
                           [sbf, ainb[j]], kk == ntap - 1)
                        if steps and kk in (4, 9, 14):
                            steps.pop(0)()
                cb = base + P_CAB + j
                op(ACT, lambda: nc.scalar.activation(out=ac_bf[:, j, :], in_=ps[:, b, :], func=AF.Identity, bias=pcol(cb), scale=1.0),
                   reads=[pb[b], constb, prmb], writes=[acb[j]])
                op(ACT, lambda: nc.scalar.activation(out=asq[:, j, :], in_=ps[:, b, :], func=AF.Square, bias=pcol(cb), scale=1.0),
                   reads=[pb[b], constb, prmb], writes=[asqb[j]])
            while steps:
                steps.pop(0)()
            if i + 1 < NT:
                op(DVE, lambda: nc.vector.tensor_copy(out=a_in[:, :, 0:30], in_=a_in[:, :, T:T + 30]), reads=[], writes=ainb)
            issue_casts(cur[0][0], cur[0][1], 1)
            bm = alloc_bank()
            reserved.add(bm)
            bq = alloc_bank()
            reserved.add(bq)
            for j in range(4):
                mm(bm, ones512[:], ac_bf[:, j, :], j == 0, j == 3, [acb[j], constb, prmb], False)
                mm(bq, ones512[:], asq[:, j, :], j == 0, j == 3, [asqb[j], constb, prmb], False)

            def ln_head():
                op(ACT, lambda: nc.scalar.activation(out=var_t[:], in_=ps[:, bm, :], func=AF.Square), reads=[pb[bm]], writes=[varb])
                op(DVE, lambda: nc.vector.tensor_tensor(out=var_t[:], in0=ps[:, bq, :], in1=var_t[:], op=ALU.subtract),
                   reads=[pb[bq], varb], writes=[varb])
                op(ACT, lambda: nc.scalar.activation(out=sd_t[:], in_=var_t[:], func=AF.Sqrt, bias=epsc[:, 0:1], scale=1.0),
                   reads=[varb, constb, prmb], writes=[sdb])
                op(DVE, lambda: nc.vector.reciprocal(out=ps[:, bq, :], in_=sd_t[:]), reads=[sdb], writes=[pb[bq]])

            def ln_j(j):
                k = next_tmp()
                op(DVE, lambda: nc.vector.tensor_tensor(out=tmpM[k][:], in0=ac_bf[:, j, :], in1=ps[:, bm, :], op=ALU.subtract),
                   reads=[acb[j], pb[bm]], writes=[tmpb[k]])
                op(DVE, lambda: nc.vector.tensor_tensor(out=tmpM[k][:], in0=tmpM[k][:], in1=ps[:, bq, :], op=ALU.mult),
                   reads=[tmpb[k], pb[bq]], writes=[tmpb[k]])
                op(ACT, lambda: nc.scalar.activation(out=ybuf[:, j, :], in_=tmpM[k][:], func=AF.Silu,
                                                     bias=pcol(base + P_LNB + j), scale=pcol(base + P_LNG + j)),
                   reads=[tmpb[k], constb, prmb], writes=[yb[j]])

            for u in range(4, 10):
                inproj_unit(u)
                if u == 4:
                    ln_head()
                elif u <= 8:
                    ln_j(u - 5)
            reserved.discard(bm)
            reserved.discard(bq)
            issue_casts(cur[0][0], cur[0][1], 2)
            slot, sbf = next_unit(sCB[l * 128:(l + 1) * 128, :], 1536, [dg_tok("B", l)])
            for j in range(4):
                b = alloc_bank()
                for kt in range(3):
                    mm(b, slot[:, (j * 3 + kt) * 128:(j * 3 + kt + 1) * 128], cx[:, j, kt:kt + T], kt == 0, kt == 2,
                       [sbf, cxb[j]], False)
                op(DVE, lambda: nc.vector.tensor_tensor(out=ybuf[:, 4 + j, :], in0=ps[:, b, :], in1=gb[:, j, :], op=ALU.mult),
                   reads=[pb[b], gbb[j]], writes=[yb[4 + j]])
            if i + 1 < NT:
                op(ACT, lambda: nc.scalar.activation(out=cx[:, :, 0:2], in_=cx[:, :, T:T + 2], func=AF.Identity), reads=[], writes=cxb)
            for f in p2s:
                f()
            x_i = xt(i)
            for u in range(4):
                slot, sbf = next_unit(sO[(l * 4 + u) * 128:(l * 4 + u + 1) * 128, :], 2048, [cast_tokens[("O", l, u)]])
                W = slot[:, 0:2048].rearrange("p (c n) -> p c n", c=NCH)
                for s in range(2):
                    oc = 2 * u + s
                    b = alloc_bank()
                    for c in range(NCH):
                        mm(b, W[:, c, s * 128:(s + 1) * 128], ybuf[:, c, :], c == 0, c == NCH - 1, [sbf, yb[c]], False)
                    op(DVE, lambda: nc.vector.tensor_tensor(out=x_i[:, oc, :], in0=ps[:, b, :], in1=x_i[:, oc, :], op=ALU.add),
                       reads=[pb[b]], writes=[xb[i]])

        def ffn_tile(l, i, steps, p2s, tail):
            units = {}
            steps = list(steps)

            cf = prm[:, l * PL + P_CFW:l * PL + P_CFW + 6 * NF].rearrange("p (q k) -> p q k", k=3)
            op(DVE, lambda: nc.vector.tensor_tensor(out=contrib[:, :, 1], in0=halu[:, :, 1], in1=cf[:, :, 0], op=ALU.mult),
               reads=halub + [prmb], writes=[contribb])
            op(DVE, lambda: nc.vector.tensor_tensor(out=contrib[:, :, 0], in0=halu[:, :, 0], in1=cf[:, :, 0], op=ALU.mult),
               reads=halub + [prmb], writes=[contribb])
            op(DVE, lambda: nc.vector.tensor_tensor(out=tmp44, in0=halu[:, :, 1], in1=cf[:, :, 1], op=ALU.mult),
               reads=halub + [prmb], writes=[contribb])
            op(DVE, lambda: nc.vector.tensor_tensor(out=contrib[:, :, 0], in0=contrib[:, :, 0], in1=tmp44, op=ALU.add),
               reads=[contribb], writes=[contribb])

            fixq = []

            def conv_taps(b, acc, accbuf, q, c0):
                op(ACT, lambda: nc.scalar.activation(out=acc[:, 0:T], in_=ps[:, b, :], func=AF.Identity, scale=pcol(c0 + 2)),
                   reads=[pb[b], prmb], writes=[accbuf])
                op(ACT, lambda: nc.scalar.activation(out=halu[:, q, :], in_=ps[:, b, T - 2:T], func=AF.Identity),
                   reads=[pb[b]], writes=[halub[q]])
                op(DVE, lambda: nc.vector.scalar_tensor_tensor(out=acc[:, 1:T], in0=ps[:, b, 0:T - 1], scalar=pcol(c0 + 1),
                                                               in1=acc[:, 1:T], op0=ALU.mult, op1=ALU.add),
                   reads=[pb[b], accbuf, prmb], writes=[accbuf])
                op(DVE, lambda: nc.vector.scalar_tensor_tensor(out=acc[:, 2:T], in0=ps[:, b, 0:T - 2], scalar=pcol(c0),
                                                               in1=acc[:, 2:T], op0=ALU.mult, op1=ALU.add),
                   reads=[pb[b], accbuf, prmb], writes=[accbuf])
                fixq.append(lambda: op(POOL, lambda: nc.gpsimd.tensor_tensor(out=acc[:, 0:2], in0=acc[:, 0:2],
                                                                             in1=contrib[:, q, :], op=ALU.add),
                                       reads=[contribb, accbuf], writes=[accbuf]))

            def up(j):
                slot, sbf = next_unit(sU[(l * NF + j) * 128:(l * NF + j + 1) * 128, 0:2048], 2048, [cast_tokens[("U", l, j)]])
                W = slot[:, 0:2048].rearrange("p (c n) -> p c n", c=NCH)
                bks = []
                for s_ in range(2):
                    b = alloc_bank()
                    for c in range(NCH):
                        mm(b, W[:, c, s_ * 128:(s_ + 1) * 128], hbuf[:, c, :], c == 0, c == NCH - 1, [sbf, hbc[c]], False)
                    bks.append(b)
                bG, bV = bks
                kg, kv = j % 3, j % 2
                c0 = l * PL + P_CFW + j * 6
                conv_taps(bG, accg[kg], accgb[kg], 2 * j, c0)
                conv_taps(bV, accv[kv], accvb[kv], 2 * j + 1, c0 + 3)
                units[j] = (kg, kv)

            def conv(j):
                kg, kv = units.pop(j)
                op(ACT, lambda: nc.scalar.activation(out=accg[kg][:], in_=accg[kg][:], func=AF.Silu),
                   reads=[accgb[kg]], writes=[accgb[kg]])
                op(POOL, lambda: nc.gpsimd.tensor_tensor(out=gbuf[:, j, :], in0=accv[kv][:], in1=accg[kg][:], op=ALU.mult),
                   reads=[accvb[kv], accgb[kg]], writes=[gfb[j]])

            for j in range(NF + 1):
                if j < NF:
                    up(j)
                if j >= 1:
                    conv(j - 1)
                while fixq:
                    fixq.pop(0)()
                if steps and j >= 2:
                    steps.pop(0)()
            while steps:
                steps.pop(0)()
            for f in p2s:
                f()
            x_i = xt(i)
            for oc in range(NCH):
                slot, sbf = next_unit(sD[(l * 8 + oc) * 128:(l * 8 + oc + 1) * 128, :], 2816, [cast_tokens[("D", l, oc)]])
                b = alloc_bank()
                for fc in range(NF):
                    mm(b, slot[:, fc * 128:(fc + 1) * 128], gbuf[:, fc, :], fc == 0, fc == NF - 1, [sbf, gfb[fc]], False)
                op(DVE, lambda: nc.vector.tensor_tensor(out=x_i[:, oc, :], in0=ps[:, b, :], in1=x_i[:, oc, :], op=ALU.add),
                   reads=[pb[b]], writes=[xb[i]])
            for f in tail:
                f()

        def barrier():
            toks = [e.last() for e in engines]
            for e in engines:
                for t in toks:
                    if t is not None and t[0] is not e.sem:
                        e.wait(t)

        stages = []
        for l in range(L):
            stages.append(("M", l, l * PL + P_G1))
            stages.append(("F", l, l * PL + P_G2))

        s_out = mksem("s_out")

        def final_p2(i):
            x_i = xt(i)
            norm_p2(("fin", i), i, P_GF, lambda c: x_i[:, c, :], lambda c: [xb[i]])

        def final_store(i):
            dma(SP, s_out, outT_v[:, :, i * T:(i + 1) * T], xt(i), reads=[xb[i]])

        prologue_builds()
        for f in norm_p1_steps((0, 0), 0):
            f()
        norm_p2((0, 0), 0, stages[0][2], h_out, lambda c: [hbc[c]])
        flush_unreserve()
        last_si = len(stages) - 1
        cur = [None]
        for si, (kind, l, gcol) in enumerate(stages):
            if kind == "M":
                op(DVE, lambda: nc.vector.memset(a_in[:, :, 0:30], 0.0), writes=ainb)
                op(DVE, lambda: nc.vector.memset(cx[:, :, 0:2], 0.0), writes=cxb)
            else:
                op(DVE, lambda: nc.vector.memset(halu, 0.0), writes=halub)
            for i in range(NT):
                cur[0] = (si, i)
                issue_casts(si, i, 0)
                steps, p2s, tail = [], [], []
                if i + 1 < NT:
                    nk, ni, ng = (si, i + 1), i + 1, gcol
                elif si + 1 < len(stages):
                    nk, ni, ng = (si + 1, 0), 0, stages[si + 1][2]
                else:
                    nk = None
                if nk is not None:
                    steps += norm_p1_steps(nk, ni)
                    p2s.append(lambda nk=nk, ni=ni, ng=ng: norm_p2(nk, ni, ng, h_out, lambda c: [hbc[c]]))
                if si == last_si and i >= 1:
                    steps += norm_p1_steps(("fin", i - 1), i - 1)
                    p2s.append(lambda i=i: final_p2(i - 1))
                    tail.append(lambda i=i: final_store(i - 1))
                if kind == "M":
                    mixer_tile(l, i, steps, p2s)
                else:
                    ffn_tile(l, i, steps, p2s, tail)
                flush_unreserve()
            barrier()
        for f in norm_p1_steps(("fin", NT - 1), NT - 1):
            f()
        final_p2(NT - 1)
        final_store(NT - 1)
        SP.wait((s_out, s_out.total))
    return nc


_CACHE = {}


def kernel(**inputs):
    x = np.asarray(inputs["x"], np.float32)
    B = x.shape[0]
    assert x.shape == (8, S, D)
    prm = _prep_params({k: np.asarray(v, np.float32) for k, v in inputs.items() if k not in ("x",)})
    wA, wO, wU, wD = _prep_weights(inputs)
    ident = np.eye(128, dtype=np.float32)
    if "nc" not in _CACHE:
        _CACHE["nc"] = build_program()
    nc = _CACHE["nc"]
    in_maps = []
    for b in range(B):
        in_maps.append({"xT": np.ascontiguousarray(x[b].T), "wA": wA, "wO": wO, "wU": wU, "wD": wD,
                        "prm": prm, "ident": ident})
    res = run_bass_kernel_spmd(nc, in_maps, core_ids=list(range(B)))
    out = np.stack([np.ascontiguousarray(res.results[b]["outT"].T) for b in range(B)], axis=0)
    return out.astype(np.float32)
```
